# Optimizing a Trainium2 kernel written in Bass

```python
import math
import jax, jax.numpy as jnp
from jax import lax
import numpy as np

D_MODEL = 2048
BATCH = 8
SEQ = 4096
DEPTH = 2

N_META = 16
D_MIX = D_MODEL
CONV_W = D_MIX // 4
CONV_K = 31
HEAD_DIM = 64
N_HEADS = (D_MIX // 2) // HEAD_DIM
N_KV = 4
GROUP = N_HEADS // N_KV
ATT_W = N_HEADS * HEAD_DIM
KV_W = N_KV * HEAD_DIM
WINDOW = 128
BLOCK = 128
ROT_DIM = HEAD_DIM // 4
ROPE_THETA = 500000.0
LRU_W = D_MIX // 4
LRU_HEADS = 8
LRU_HEAD_DIM = LRU_W // LRU_HEADS
LRU_CONV_K = 4
LRU_C = 8.0
IN_WIDTHS = (CONV_W, CONV_W, CONV_W,
             ATT_W, KV_W, KV_W, ATT_W,
             LRU_W, LRU_W)
IN_TOTAL = sum(IN_WIDTHS)
OUT_IN = CONV_W + ATT_W + LRU_W
LN_EPS = 1e-5
DEEPNORM_ALPHA = (2.0 * DEPTH) ** 0.25
DEEPNORM_BETA = (8.0 * DEPTH) ** -0.25
NEG_INF = -1e30

kernel_name = "hymba_style_conv_swa_rglru_deepnorm"


def _layer_norm(x, g, b):
    xf = x.astype(jnp.float32)
    mu = jnp.mean(xf, axis=-1, keepdims=True)
    var = jnp.mean(jnp.square(xf - mu), axis=-1, keepdims=True)
    y = (xf - mu) * lax.rsqrt(var + LN_EPS)
    return (y * g.astype(jnp.float32) + b.astype(jnp.float32)).astype(x.dtype)


def _causal_depthwise_conv(x, w, b):
    k = w.shape[0]
    y = lax.conv_general_dilated(
        x, w[:, None, :].astype(x.dtype), window_strides=(1,), padding=[(k - 1, 0)],
        dimension_numbers=("NWC", "WIO", "NWC"), feature_group_count=x.shape[-1])
    return y + b


def _partial_rotary(x, pos):
    half = ROT_DIM // 2
    inv_freq = ROPE_THETA ** (-jnp.arange(half, dtype=jnp.float32) / half)
    ang = pos.astype(jnp.float32)[:, None] * inv_freq[None, :]
    cos = jnp.cos(ang)[None, :, None, :]
    sin = jnp.sin(ang)[None, :, None, :]
    x1 = x[..., :half].astype(jnp.float32)
    x2 = x[..., half:ROT_DIM].astype(jnp.float32)
    rot = jnp.concatenate([x1 * cos - x2 * sin, x2 * cos + x1 * sin], axis=-1).astype(x.dtype)
    return jnp.concatenate([rot, x[..., ROT_DIM:]], axis=-1)


def _sliding_window_sink_attention(q, k, v, sinks):
    B, L, H, Dh = q.shape
    S = L - N_META
    nblk = S // BLOCK
    scale = Dh ** -0.5
    sink_kg = sinks.astype(jnp.float32).reshape(N_KV, GROUP)

    qm, qr = q[:, :N_META], q[:, N_META:]
    km, kr = k[:, :N_META], k[:, N_META:]
    vm, vr = v[:, :N_META], v[:, N_META:]

    qb = qr.reshape(B, nblk, BLOCK, N_KV, GROUP, Dh)
    kb = kr.reshape(B, nblk, BLOCK, N_KV, Dh)
    vb = vr.reshape(B, nblk, BLOCK, N_KV, Dh)
    pad = ((0, 0), (1, 0), (0, 0), (0, 0), (0, 0))
    k_band = jnp.concatenate([jnp.pad(kb, pad)[:, :-1], kb], axis=2)
    v_band = jnp.concatenate([jnp.pad(vb, pad)[:, :-1], vb], axis=2)

    s_meta = jnp.einsum("bnqkgd,bmkd->bkgnqm", qb, km).astype(jnp.float32) * scale
    s_band = jnp.einsum("bnqkgd,bnjkd->bkgnqj", qb, k_band).astype(jnp.float32) * scale
    qi = jnp.arange(BLOCK)[:, None]
    kj = jnp.arange(2 * BLOCK)[None, :]
    diff = BLOCK + qi - kj
    in_window = (diff >= 0) & (diff < WINDOW)
    blk = jnp.arange(nblk)[:, None, None]
    valid = in_window[None] & ((kj[None] >= BLOCK) | (blk > 0))
    s_band = jnp.where(valid[None, None, None], s_band, NEG_INF)
    sink_col = jnp.broadcast_to(sink_kg[None, :, :, None, None, None], s_meta.shape[:-1] + (1,))
    probs = jax.nn.softmax(jnp.concatenate([s_meta, s_band, sink_col], axis=-1), axis=-1)
    p_meta = probs[..., :N_META].astype(v.dtype)
    p_band = probs[..., N_META:N_META + 2 * BLOCK].astype(v.dtype)
    out_r = (jnp.einsum("bkgnqm,bmkd->bnqkgd", p_meta, vm)
             + jnp.einsum("bkgnqj,bnjkd->bnqkgd", p_band, v_band)).reshape(B, S, H, Dh)

    qmg = qm.reshape(B, N_META, N_KV, GROUP, Dh)
    s_mm = jnp.einsum("bqkgd,bmkd->bkgqm", qmg, km).astype(jnp.float32) * scale
    causal = jnp.tril(jnp.ones((N_META, N_META), dtype=bool))
    s_mm = jnp.where(causal[None, None, None], s_mm, NEG_INF)
    sink_mm = jnp.broadcast_to(sink_kg[None, :, :, None, None], s_mm.shape[:-1] + (1,))
    p_mm = jax.nn.softmax(jnp.concatenate([s_mm, sink_mm], axis=-1), axis=-1)[..., :N_META]
    out_m = jnp.einsum("bkgqm,bmkd->bqkgd", p_mm.astype(v.dtype), vm).reshape(B, N_META, H, Dh)

    return jnp.concatenate([out_m, out_r], axis=1)


def _rg_lru(x, w_a, b_a, w_x, b_x, lam):
    B, L, C = x.shape
    xh = x.reshape(B, L, LRU_HEADS, LRU_HEAD_DIM)
    r = jax.nn.sigmoid(jnp.einsum("blhi,hij->blhj", xh, w_a).reshape(B, L, C) + b_a)
    i = jax.nn.sigmoid(jnp.einsum("blhi,hij->blhj", xh, w_x).reshape(B, L, C) + b_x)
    log_a = -LRU_C * r.astype(jnp.float32) * jax.nn.softplus(-lam.astype(jnp.float32))
    a = jnp.exp(log_a)
    u = jnp.sqrt(-jnp.expm1(2.0 * log_a)) * (i * x).astype(jnp.float32)

    def combine(c1, c2):
        a1, b1 = c1
        a2, b2 = c2
        return a1 * a2, a2 * b1 + b2

    _, h = lax.associative_scan(combine, (a, u), axis=1)
    return h.astype(x.dtype)


def _hybrid_layer(h, pos, w_in, conv_dw_w, conv_dw_b, conv_ln_g, conv_ln_b, conv_pw_w, conv_pw_b,
                  attn_sinks, lru_conv_w, lru_conv_b, lru_wa, lru_ba, lru_wx, lru_bx, lru_lambda,
                  w_out, ln_post_g, ln_post_b):
    B, L, _ = h.shape
    proj = h @ w_in
    split_pts = [int(s) for s in np.cumsum(IN_WIDTHS)[:-1]]
    (c_val, c_glu, c_gate, q, k, v, a_gate, r_x, r_gate) = jnp.split(proj, split_pts, axis=-1)

    c = c_val * jax.nn.sigmoid(c_glu)
    c = _causal_depthwise_conv(c, conv_dw_w, conv_dw_b)
    c = jax.nn.silu(_layer_norm(c, conv_ln_g, conv_ln_b))
    c = c @ conv_pw_w + conv_pw_b
    y_conv = c * jax.nn.silu(c_gate)

    q = _partial_rotary(q.reshape(B, L, N_HEADS, HEAD_DIM), pos)
    k = _partial_rotary(k.reshape(B, L, N_KV, HEAD_DIM), pos)
    v = v.reshape(B, L, N_KV, HEAD_DIM)
    att = _sliding_window_sink_attention(q, k, v, attn_sinks).reshape(B, L, ATT_W)
    y_attn = att * jax.nn.silu(a_gate)

    r = _causal_depthwise_conv(r_x, lru_conv_w, lru_conv_b)
    r = _rg_lru(r, lru_wa, lru_ba, lru_wx, lru_bx, lru_lambda)
    y_lru = r * jax.nn.silu(r_gate)

    mixed = jnp.concatenate([y_conv, y_attn, y_lru], axis=-1) @ w_out
    return _layer_norm(DEEPNORM_ALPHA * h + mixed, ln_post_g, ln_post_b)


def setup_inputs(seed: int = 0) -> dict:
    key = jax.random.key(seed)
    ks = jax.random.split(key, 24)
    f32 = jnp.float32
    nrm = lambda k, shape, s: jax.random.normal(k, shape, f32) * s
    u = jax.random.uniform(ks[17], (DEPTH, LRU_W), f32, 0.9, 0.999)
    s = u ** (1.0 / LRU_C)
    lam = jnp.log(s) - jnp.log1p(-s)
    return {
        "x": nrm(ks[0], (BATCH, SEQ, D_MODEL), 1.0),
        "meta_tokens": nrm(ks[1], (N_META, D_MODEL), 1.0),
        "ln_in_g": 1.0 + nrm(ks[2], (D_MODEL,), 0.02),
        "ln_in_b": nrm(ks[3], (D_MODEL,), 0.02),
        "w_in": nrm(ks[4], (DEPTH, D_MODEL, IN_TOTAL), D_MODEL ** -0.5),
        "conv_dw_w": nrm(ks[5], (DEPTH, CONV_K, CONV_W), CONV_K ** -0.5),
        "conv_dw_b": nrm(ks[6], (DEPTH, CONV_W), 0.01),
        "conv_ln_g": 1.0 + nrm(ks[7], (DEPTH, CONV_W), 0.02),
        "conv_ln_b": nrm(ks[8], (DEPTH, CONV_W), 0.02),
        "conv_pw_w": nrm(ks[9], (DEPTH, CONV_W, CONV_W), DEEPNORM_BETA * CONV_W ** -0.5),
        "conv_pw_b": nrm(ks[10], (DEPTH, CONV_W), 0.01),
        "attn_sinks": nrm(ks[11], (DEPTH, N_HEADS), 0.5),
        "lru_conv_w": nrm(ks[12], (DEPTH, LRU_CONV_K, LRU_W), LRU_CONV_K ** -0.5),
        "lru_conv_b": nrm(ks[13], (DEPTH, LRU_W), 0.01),
        "lru_wa": nrm(ks[14], (DEPTH, LRU_HEADS, LRU_HEAD_DIM, LRU_HEAD_DIM), LRU_HEAD_DIM ** -0.5),
        "lru_ba": nrm(ks[15], (DEPTH, LRU_W), 0.01),
        "lru_wx": nrm(ks[16], (DEPTH, LRU_HEADS, LRU_HEAD_DIM, LRU_HEAD_DIM), LRU_HEAD_DIM ** -0.5),
        "lru_bx": nrm(ks[18], (DEPTH, LRU_W), 0.01),
        "lru_lambda": lam,
        "w_out": nrm(ks[19], (DEPTH, OUT_IN, D_MODEL), DEEPNORM_BETA * OUT_IN ** -0.5),
        "ln_post_g": 1.0 + nrm(ks[20], (DEPTH, D_MODEL), 0.02),
        "ln_post_b": nrm(ks[21], (DEPTH, D_MODEL), 0.02),
    }


def reference(x, meta_tokens, ln_in_g, ln_in_b, w_in, conv_dw_w, conv_dw_b, conv_ln_g, conv_ln_b,
              conv_pw_w, conv_pw_b, attn_sinks, lru_conv_w, lru_conv_b, lru_wa, lru_ba, lru_wx,
              lru_bx, lru_lambda, w_out, ln_post_g, ln_post_b):
    B = x.shape[0]
    meta = jnp.broadcast_to(meta_tokens[None].astype(x.dtype), (B, N_META, x.shape[-1]))
    h = jnp.concatenate([meta, x], axis=1)
    h = _layer_norm(h, ln_in_g, ln_in_b)
    pos = jnp.arange(h.shape[1], dtype=jnp.int32)
    for l in range(DEPTH):
        h = _hybrid_layer(h, pos, w_in[l], conv_dw_w[l], conv_dw_b[l], conv_ln_g[l], conv_ln_b[l],
                          conv_pw_w[l], conv_pw_b[l], attn_sinks[l], lru_conv_w[l], lru_conv_b[l],
                          lru_wa[l], lru_ba[l], lru_wx[l], lru_bx[l], lru_lambda[l],
                          w_out[l], ln_post_g[l], ln_post_b[l])
    return h[:, N_META:]
```

```python
import math
from contextlib import ExitStack

import numpy as np
import concourse.bass as bass
import concourse.mybir as mybir
from concourse.bass_utils import run_bass_kernel_spmd

F32 = mybir.dt.float32
BF16 = mybir.dt.bfloat16
I32 = mybir.dt.int32
AF = mybir.ActivationFunctionType
ALU = mybir.AluOpType

D = 2048
NKC = 16
NMETA = 16
TT = 512
SEQ = 4096
DEPTH = 2
ALPHA = (2.0 * DEPTH) ** 0.25
LN_EPS = 1e-5
GROUP_ORDER = [2, 0, 3, 1, 16, 17, 18, 19, 6, 7, 8, 9, 10, 11, 12, 13, 14, 15, 4, 5]
NPCL = 44
MASKV = -30000.0
SEM_LIMIT = 4000


class _Rec:
    def __init__(self):
        self.calls = []

    def __getattr__(self, name):
        def f(*a, **kw):
            self.calls.append((name, a, kw))
            return self
        return f


def _free_size(ap):
    n = 1
    for d in list(ap.shape)[1:]:
        n *= int(d)
    return n


def _est_cost(e, call):
    name, a, kw = call
    out = kw.get("out", a[0] if a else None)
    n = _free_size(out) if out is not None else 64
    if e == "pe":
        c = max(n, 64) / 2.3 + 12.0
        lhs = kw.get("lhsT", None)
        if (name == "transpose" and a[1].dtype == F32) or (lhs is not None and lhs.dtype == F32):
            c *= 4.0
        return c
    if e == "dve":
        c = (n + 58) / 0.87
        if name == "reciprocal":
            c *= 5.0
        if name == "tensor_tensor_scan":
            c *= 2.0
        return c
    if e == "act":
        return (n + 110) / 1.2
    if e == "pool":
        return n * 1.73 + 450.0
    return 50.0


class K:
    def __init__(self, nc):
        self.nc = nc
        self.es = ExitStack()
        self.eng = {"pe": nc.tensor, "act": nc.scalar, "dve": nc.vector, "pool": nc.gpsimd, "sp": nc.sync}
        self.ops = []
        self.last_w = {}
        self.readers = {}
        self.sems = {e: [] for e in self.eng}

    def sb(self, name, shape, dt):
        return self.es.enter_context(self.nc.sbuf_tensor("sb_" + name, list(shape), dt))

    def ps(self, name, shape, dt):
        return self.es.enter_context(self.nc.psum_tensor("ps_" + name, list(shape), dt))

    def sem(self, name):
        return self.es.enter_context(self.nc.semaphore(name))

    def _record(self, op, r, w, pri=None):
        idx = len(self.ops)
        deps = {}
        for reg in r:
            t = self.last_w.get(reg)
            if t is not None:
                deps[t] = "raw"
        for reg in w:
            t = self.last_w.get(reg)
            if t is not None:
                deps.setdefault(t, "waw")
            for t in self.readers.get(reg, ()):
                deps.setdefault(t, "war")
        op["deps"] = deps
        op["pri"] = pri if pri is not None else (0 if any(isinstance(x, str) and x[0] == "P" and x[1:].isdigit() for x in r) else 1)
        self.ops.append(op)
        for reg in w:
            self.last_w[reg] = idx
            self.readers[reg] = []
        for reg in r:
            self.readers.setdefault(reg, []).append(idx)
        return idx

    def op(self, e, emit, r=(), w=(), cost=None, pri=None):
        rec = _Rec()
        emit(rec)
        assert len(rec.calls) == 1
        return self._record(dict(e=e, call=rec.calls[0], dma=None, cost=cost if cost is not None else _est_cost(e, rec.calls[0]), tag=getattr(self, "cur_tag", "")), r, w, pri=pri)

    def dma(self, out, in_, sem, grp, r=(), w=(), q="sp"):
        nbytes = _free_size(out) * int(out.shape[0]) * (2 if out.dtype == BF16 else 4)
        return self._record(dict(e=q, call=("dma_start", (), dict(out=out, in_=in_)), dma=(sem, grp), cost=2000.0 + nbytes / 120.0), r, w)

    def wait_final(self, q, sem, val):
        self._record(dict(e=q, call=None, dma=None, cost=10.0, final=(sem, val)), (), ())
        self.ops[-1]["deps"] = {i: "raw" for i, o in enumerate(self.ops[:-1]) if o["dma"] is not None and o["dma"][0] is sem}

    def finish(self, window=256):
        ops = self.ops
        n = len(ops)
        engines = list(self.eng)
        pending = {e: [i for i in range(n) if ops[i]["e"] == e] for e in engines}
        head = {e: 0 for e in engines}
        done = [False] * n
        fin = [0.0] * n
        free = {e: 0.0 for e in engines}
        order = {e: [] for e in engines}
        remaining = n
        while remaining:
            best = None
            for e in engines:
                lst = pending[e]
                hd = head[e]
                while hd < len(lst) and done[lst[hd]]:
                    hd += 1
                head[e] = hd
                if hd >= len(lst):
                    continue
                win = 1 if e == "sp" else window
                cnt = 0
                p = hd
                cand = None
                while p < len(lst) and cnt < win:
                    i = lst[p]
                    p += 1
                    if done[i]:
                        continue
                    cnt += 1
                    ok = True
                    st = free[e]
                    for j in ops[i]["deps"]:
                        if not done[j]:
                            ok = False
                            break
                        if fin[j] > st:
                            st = fin[j]
                    if not ok:
                        continue
                    key = (st if st > free[e] else free[e], ops[i]["pri"], i)
                    if cand is None or key < cand:
                        cand = key
                    if st <= free[e] + 1e-9 and ops[i]["pri"] == 0:
                        break
                if cand is not None and (best is None or (cand[0], cand[2]) < (best[0], best[1])):
                    best = (cand[0], cand[2], e)
            assert best is not None, "scheduler deadlock"
            st, i, e = best
            o = ops[i]
            done[i] = True
            remaining -= 1
            if o["dma"] is not None:
                free[e] = st + 60.0
                fin[i] = st + o["cost"]
            else:
                free[e] = st + o["cost"]
                fin[i] = free[e] + 60.0
            order[e].append(i)
            if getattr(self, "trace", None) is not None:
                self.trace.append((e, st, o["cost"], i, (o["call"][0] if o["call"] else "final") + ":" + o.get("tag", "")))
        self.est_ns = max(fin) if n else 0.0
        busy = {e: sum(ops[i]["cost"] for i in order[e] if ops[i]["dma"] is None) for e in engines}
        print("[sched] est total us %.1f" % (self.est_ns / 1e3), {e: round(b / 1e3, 1) for e, b in busy.items()}, "n_ops", n, flush=True)
        seqno = {}
        for e in engines:
            s = 0
            for i in order[e]:
                if ops[i]["dma"] is None and ops[i]["call"] is not None:
                    seqno[i] = s
                    s += 1
        for e in engines:
            ngen = (sum(1 for i in order[e] if i in seqno) + SEM_LIMIT - 1) // SEM_LIMIT
            self.sems[e] = [self.sem(f"s_{e}_{g}") for g in range(ngen)]
        for e in engines:
            eng = self.eng[e]
            waited = {}
            for i in order[e]:
                o = ops[i]
                for j, kind in o["deps"].items():
                    d = ops[j]
                    if d["dma"] is None and d["call"] is None:
                        continue
                    if d["dma"] is None and d["e"] == e:
                        if e == "pe":
                            continue
                    if d["dma"] is not None:
                        sem, grp = d["dma"]
                        val = grp[0]
                        assert val is not None
                        key = ("dma", sem.num)
                    else:
                        sq = seqno[j]
                        key = ("eng", d["e"], sq // SEM_LIMIT)
                        sem = self.sems[d["e"]][sq // SEM_LIMIT]
                        val = sq % SEM_LIMIT + 1
                        skip = False
                        for g2 in range(sq // SEM_LIMIT + 1, len(self.sems[d["e"]])):
                            if ("eng", d["e"], g2) in waited:
                                skip = True
                        if skip:
                            continue
                    if waited.get(key, -1) >= val:
                        continue
                    waited[key] = val
                    eng.wait_ge(sem, val)
                if o.get("final") is not None:
                    sem, val = o["final"]
                    eng.wait_ge(sem, val)
                    continue
                name, a, kw = o["call"]
                ins = getattr(eng, name)(*a, **kw)
                if o["dma"] is not None:
                    ins.then_inc(o["dma"][0], 16)
                else:
                    sq = seqno[i]
                    ins.then_inc(self.sems[e][sq // SEM_LIMIT], 1)


class DmaSem:
    def __init__(self, k, name):
        self.k = k
        self.sem = k.sem(name)
        self.n = 0
        self.grp = None

    def group(self):
        self.grp = [None]

    def end(self):
        self.grp[0] = 16 * self.n
        self.grp = None

    def go(self, out, in_, r=(), w=(), q="sp"):
        self.n += 1
        g = self.grp if self.grp is not None else [16 * self.n]
        return self.k.dma(out, in_, self.sem, g, r=r, w=w, q=q)

    def final(self):
        return ("dma", self.sem, [16 * self.n])


def head_of(tile, half):
    if tile < 4:
        return tile if half == 0 else 4 + tile
    return 8 + (tile - 4) if half == 0 else 12 + (tile - 4)


def attn_perm():
    idx = []
    for t in range(8):
        for hh in range(2):
            h = head_of(t, hh)
            idx.extend(range(h * 64, h * 64 + 64))
    return np.array(idx)


def pc_index(l, c, j):
    return (l * 4 + c) * NPCL + j


def build(n_real=8, dbg=False):
    S = n_real * TT
    L = NMETA + S
    nc = bass.Bass("TRN2", target_bir_lowering=False)

    def din(name, shape, dt=F32):
        return nc.dram_tensor(name, list(shape), dt, kind="ExternalInput").ap()

    x_d = din("x", [S, D])
    meta_d = din("meta", [NMETA, D])
    lnp_d = din("lnp", [3, 128, 2, D])
    win_d = din("w_in_r", [2, 20, 128, NKC * 256])
    wout_d = din("w_out_r", [2, 8, 128, NKC * 256])
    pw_d = din("pw_r", [2, 128, 4 * 512])
    bd_d = din("lru_bd", [2, 128, 2 * 4 * 128])
    pc_d = din("pc", [128, 2 * 4 * NPCL])
    sk_d = din("sinks_rep", [1, 32])
    ident_d = din("ident", [128, 128])
    perm_d = din("perm", [128, 128])
    maskb_d = din("maskb", [128, 2 * 128])
    maskm_d = din("maskm", [16, 64])
    pos_d = din("pos", [128, L])
    rc_d = din("rotc", [128, 2])
    srow_d = din("sinkrow", [1, 2 * 128])
    y_d = nc.dram_tensor("y", [S, D], F32, kind="ExternalOutput").ap()
    winb_d = nc.dram_tensor("w_in_b", [2, 20, 128, NKC * 256], BF16, kind="Internal").ap()
    woutb_d = nc.dram_tensor("w_out_b", [2, 8, 128, NKC * 256], BF16, kind="Internal").ap()
    pwb_d = nc.dram_tensor("pw_b16", [2, 128, 4 * 512], BF16, kind="Internal").ap()
    bdb_d = nc.dram_tensor("bd_b16", [2, 128, 2 * 4 * 128], BF16, kind="Internal").ap()
    dbg_d = {}
    if dbg:
        for nm, shp in (("d_h0", [128, 4 * D]), ("d_yT", [128, NKC * TT]), ("d_h1", [128, 4 * D])):
            dbg_d[nm] = nc.dram_tensor(nm, shp, F32 if nm != "d_yT" else BF16, kind="ExternalOutput").ap()

    k = K(nc)
    with k.es:
        h = k.sb("h", [128, 4, D], F32)
        hT = k.sb("hT", [128, NKC, TT], BF16)
        yT = k.sb("yT", [128, NKC, TT], BF16)
        _hTf = hT[:].rearrange("p a b -> p (a b)").bitcast(F32)
        _yTf = yT[:].rearrange("p a b -> p (a b)").bitcast(F32)
        xs = [_hTf[:, 0:D], _hTf[:, D:2 * D], _yTf[:, 0:D], _yTf[:, D:2 * D]]
        xs_reg = [[("hT", kc) for kc in range(0, 8)], [("hT", kc) for kc in range(8, 16)],
                  [("yT", kc) for kc in range(0, 8)], [("yT", kc) for kc in range(8, 16)]]
        NSLOT = 3
        wbuf = [k.sb(f"wbuf{i}", [128, NKC, 256], BF16) for i in range(NSLOT)]
        lnp = k.sb("lnp", [128, 2, D], F32)
        ident = k.sb("ident", [128, 128], F32)
        identb = k.sb("identb", [128, 128], BF16)
        permb = k.sb("permb", [128, 128], BF16)
        onesf = k.sb("onesf", [128, 128], F32)
        maskb = k.sb("maskb", [128, 2, 128], BF16)
        maskm = k.sb("maskm", [16, 64], BF16)
        pc = k.sb("pc", [128, 2 * 4 * NPCL], F32)
        pch = k.sb("pch", [128, 2 * 4 * NPCL], F32)
        hc = k.sb("hc", [128, 8], F32)
        nhc = k.sb("nhc", [128, 8], F32)
        rotc = k.sb("rotc", [128, 2], F32)
        sinkp = k.sb("sinkp", [1, 32], BF16)
        srow = k.sb("srow", [1, 2 * 128], BF16)
        pwb = k.sb("pwb", [128, 4, 512], BF16)
        bdb = k.sb("bdb", [128, 2, 4, 128], BF16)
        Ct = k.sb("Ct", [128, TT], F32)
        St = k.sb("St", [128, TT], F32)
        cv_halo = k.sb("cv_halo", [128, 2, 4, 30], F32)
        lc_halo = k.sb("lc_halo", [128, 2, 4, 3], F32)
        lru_st = k.sb("lru_st", [128, 2, 4], F32)
        kprev = k.sb("kprev", [128, 2, 2, 128], BF16)
        kmeta = k.sb("kmeta", [128, 2, 2, 16], BF16)
        vprev = k.sb("vprev", [128, 2, 4, 128], BF16)
        vmeta = k.sb("vmeta", [16, 2, 4, 128], BF16)
        NF = 22
        sga = k.sb("sga", [128, 8, TT], F32)
        F = [k.sb(f"F{i}", [128, TT], F32) if not (4 <= i < 12) else None for i in range(NF)]
        cg = k.sb("cg", [128, 4, 30 + TT], F32)
        lx = k.sb("lx", [128, 4, 3 + TT], F32)
        qT = k.sb("qT", [128, 8, TT], BF16)
        kT = k.sb("kT", [128, 2, 128 + TT], BF16)
        vaug = k.sb("vaug", [128, 4, 4, 128], BF16)
        NB = 6
        B = [k.sb(f"B{i}", [128, TT], BF16) for i in range(NB)]
        stat = k.sb("stat", [128, 4, 24], F32)
        mv = k.sb("mv", [128, 4, 2], F32)
        sm = k.sb("sm", [128, 16], F32)
        banks = [k.ps(f"P{i}", [128, 512], F32) for i in range(8)]
        bank_i = [0]

        aux_i = [0]

        def bank(aux=False):
            if aux:
                i = 4 + aux_i[0] % 2
                aux_i[0] += 1
            else:
                i = bank_i[0] % 4
                bank_i[0] += 1
            return banks[i], f"P{i}"

        ld = DmaSem(k, "ld")

        ldx = [DmaSem(k, f"ldx{i}") for i in range(4)]
        ldln = DmaSem(k, "ldln")
        ldpos = DmaSem(k, "ldpos")
        ld.group()
        ld.go(ident[:], ident_d, w=["ident"])
        ld.go(pc[:], pc_d, w=["pc"])
        ld.go(rotc[:], rc_d, w=["rotc"])
        ld.go(F[1][0:1, 0:32], sk_d, w=["F1"])
        ld.end()
        cst = DmaSem(k, "cst")
        cst.group()
        cst.go(permb[:], perm_d, w=["permb"], q="pool")
        cst.go(identb[:], ident_d, w=["identb"], q="pool")
        cst.go(maskb[:].rearrange("p a b -> p (a b)"), maskb_d, w=["maskb"], q="pool")
        cst.go(maskm[:], maskm_d, w=["maskm"], q="pool")
        cst.go(srow[:], srow_d, w=["srow"], q="pool")
        cst.end()
        smallw = DmaSem(k, "smallw")
        smallw.group()
        for l in range(2):
            smallw.go(pwb_d[l], pw_d[l], w=[("pwb_d", l)], q="pool")
            smallw.go(bdb_d[l], bd_d[l], w=[("bdb_d", l)], q="pool")
        smallw.end()
        ldsw = DmaSem(k, "ldsw")
        NPS = 8
        prep_sems = [DmaSem(k, f"prep{i}") for i in range(NPS)]
        prep_tok = {}
        pi = 0
        for l in range(2):
            for g in GROUP_ORDER:
                prep_tok[("in", l, g)] = prep_sems[pi % NPS].go(winb_d[l, g], win_d[l, g], w=[("winb", l, g), ("prepslot", pi % NPS)], q="pool")
                pi += 1
            for g in range(8):
                prep_tok[("out", l, g)] = prep_sems[pi % NPS].go(woutb_d[l, g], wout_d[l, g], w=[("woutb", l, g), ("prepslot", pi % NPS)], q="pool")
                pi += 1
        k.op("dve", lambda e: e.memset(onesf[:], 1.0 / 512.0), w=["onesf"])
        k.op("dve", lambda e: e.memset(cv_halo[:].rearrange("p a b c -> p (a b c)"), 0.0), w=["cv_halo0", "cv_halo1"])
        k.op("dve", lambda e: e.memset(lc_halo[:].rearrange("p a b c -> p (a b c)"), 0.0), w=["lc_halo0", "lc_halo1"])
        k.op("dve", lambda e: e.memset(lru_st[:].rearrange("p a b -> p (a b)"), 0.0), w=["lru_st0", "lru_st1"])
        k.op("pool", lambda e: e.memset(vaug[:].rearrange("p a b c -> p (a b c)"), 1.0), w=["vaug"])
        k.op("pool", lambda e: e.memset(vprev[:].rearrange("p a b c -> p (a b c)"), 1.0), w=["vprev0", "vprev1"])
        k.op("pool", lambda e: e.memset(vmeta[:].rearrange("p a b c -> p (a b c)"), 1.0), w=["vmeta0", "vmeta1"])
        k.op("dve", lambda e: e.tensor_scalar(out=pch[:], in0=pc[:], scalar1=0.5, scalar2=None, op0=ALU.mult),
             r=["pc"], w=["pch"])
        lam_ap = pc[:].rearrange("p (x j) -> p x j", j=NPCL)[:, :, 42]
        e_t, e2, acc_t = sm[:, 0:8], sm[:, 8:16], F[0][:, 0:8]
        k.op("act", lambda e: e.activation(out=e_t, in_=lam_ap, func=AF.Exp, scale=-1.0), r=["pc"], w=["sm"])
        k.op("dve", lambda e: e.tensor_scalar(out=acc_t, in0=e_t, scalar1=-0.25, scalar2=1.0 / 3.0, op0=ALU.mult, op1=ALU.add),
             r=["sm"], w=["F0"])
        k.op("dve", lambda e: e.tensor_tensor(out=e2, in0=e_t, in1=acc_t, op=ALU.mult), r=["sm", "F0"], w=["sm2"])
        k.op("dve", lambda e: e.tensor_scalar(out=acc_t, in0=e2, scalar1=-1.0, scalar2=0.5, op0=ALU.mult, op1=ALU.add),
             r=["sm2"], w=["F0"])
        k.op("dve", lambda e: e.tensor_tensor(out=e2, in0=e_t, in1=acc_t, op=ALU.mult), r=["sm", "F0"], w=["sm2"])
        k.op("dve", lambda e: e.tensor_scalar(out=acc_t, in0=e2, scalar1=-1.0, scalar2=1.0, op0=ALU.mult, op1=ALU.add),
             r=["sm2"], w=["F0"])
        k.op("dve", lambda e: e.tensor_tensor(out=e2, in0=e_t, in1=acc_t, op=ALU.mult), r=["sm", "F0"], w=["sm2"])
        k.op("dve", lambda e: e.tensor_scalar(out=hc[:], in0=e2, scalar1=-4.0, scalar2=None, op0=ALU.mult), r=["sm2"], w=["hc"])
        k.op("dve", lambda e: e.tensor_scalar(out=nhc[:], in0=e2, scalar1=4.0, scalar2=None, op0=ALU.mult), r=["sm2"], w=["nhc"])
        k.op("act", lambda e: e.activation(out=sinkp[0:1, :], in_=F[1][0:1, 0:32], func=AF.Exp), r=["F1"], w=["sinkp"])
        items = []
        tiles = [("meta", NMETA, 0, 0)] + [("real", TT, NMETA + TT * i, i) for i in range(n_real)]
        for _ in tiles:
            for l in range(2):
                for g in GROUP_ORDER:
                    items.append(("in", l, g))
                for g in range(8):
                    items.append(("out", l, g))
        wsem = [DmaSem(k, f"w{i}") for i in range(NSLOT)]
        wstate = {"next_load": 0, "next_use": 0}

        def w_issue():
            n = wstate["next_load"]
            if n >= len(items):
                return
            kind, l, g = items[n]
            slot = n % NSLOT
            src = (winb_d if kind == "in" else woutb_d)[l, g]
            reg = ("winb", l, g) if kind == "in" else ("woutb", l, g)
            wsem[slot].go(wbuf[slot][:].rearrange("p a b -> p (a b)"), src, r=[reg], w=[f"wbuf{slot}"])
            wstate["next_load"] = n + 1

        def w_get(expect):
            n = wstate["next_use"]
            assert items[n] == expect, (items[n], expect)
            wstate["next_use"] = n + 1
            slot = n % NSLOT
            return wbuf[slot], f"wbuf{slot}"

        for _ in range(NSLOT):
            w_issue()

        def pcs(l, c, j, half=False):
            i = pc_index(l, c, j)
            return (pch if half else pc)[:, i:i + 1]

        lnp_loaded = [False]
        unit_i = [0]
        ln_calls = [0]

        def layer_norm_rows(T, nsub, P, gb_idx, stats_done=False, staged=False):
            if not lnp_loaded[0]:
                ldln.go(lnp[:].rearrange("p a b -> p (a b)"), lnp_d[gb_idx].rearrange("p a b -> p (a b)"), w=["lnp"])
                lnp_loaded[0] = True
            for s in range(nsub):
                if not stats_done:
                    for c4 in range(4):
                        src_ap = xs[s][0:P, c4 * 512:(c4 + 1) * 512] if staged else h[0:P, s, c4 * 512:(c4 + 1) * 512]
                        k.op("dve", lambda e, s=s, c4=c4, src_ap=src_ap: e.bn_stats(out=stat[0:P, s, c4 * 6:(c4 + 1) * 6], in_=src_ap),
                             r=(xs_reg[s] if staged else [("h", s)]), w=[("stat", s, c4)])
                    k.op("dve", lambda e, s=s: e.bn_aggr(out=mv[0:P, s, :], in_=stat[0:P, s, 0:24]),
                         r=[("stat", s, c4) for c4 in range(4)], w=[("mv", s)])
                else:
                    k.op("dve", lambda e, s=s: e.bn_aggr(out=mv[0:P, s, :], in_=stat[0:P, s, 0:24]),
                         r=[("stat", s, c4) for c4 in range(4)], w=[("mv", s)])
            k.op("dve", lambda e: e.tensor_scalar(out=sm[0:P, 0:nsub], in0=mv[0:P, 0:nsub, 1], scalar1=LN_EPS, scalar2=None, op0=ALU.add),
                 r=[("mv", s) for s in range(nsub)], w=["sm"])
            k.op("act", lambda e: e.activation(out=sm[0:P, 4:4 + nsub], in_=sm[0:P, 0:nsub], func=AF.Sqrt), r=["sm"], w=["sm_b"])
            k.op("dve", lambda e: e.reciprocal(out=sm[0:P, 8:8 + nsub], in_=sm[0:P, 4:4 + nsub]), r=["sm_b"], w=["sm_c"])
            k.op("dve", lambda e: e.scalar_tensor_tensor(out=sm[0:P, 12:12 + nsub], in0=mv[0:P, 0:nsub, 0], scalar=-1.0,
                                                         in1=sm[0:P, 8:8 + nsub], op0=ALU.mult, op1=ALU.mult),
                 r=["sm_c"] + [("mv", s) for s in range(nsub)], w=["sm_d"])
            for s in range(nsub):
                src_ap = xs[s][0:P, :] if staged else h[0:P, s, :]
                k.op("act", lambda e, s=s, src_ap=src_ap: e.activation(out=h[0:P, s, :], in_=src_ap, func=AF.Identity,
                                                        scale=sm[0:P, 8 + s:9 + s], bias=sm[0:P, 12 + s:13 + s]),
                     r=(xs_reg[s] if staged else [("h", s)]) + ["sm_c", "sm_d"], w=[("h", s)])
                k.op("dve", lambda e, s=s: e.tensor_tensor(out=h[0:P, s, :], in0=h[0:P, s, :], in1=lnp[0:P, 0, :], op=ALU.mult),
                     r=[("h", s), "lnp"], w=[("h", s)])
                k.op("dve", lambda e, s=s: e.tensor_tensor(out=h[0:P, s, :], in0=h[0:P, s, :], in1=lnp[0:P, 1, :], op=ALU.add),
                     r=[("h", s), "lnp"], w=[("h", s)])
            nxt = (gb_idx + 1) % 3
            ln_calls[0] += 1
            if ln_calls[0] < 3 * len(tiles):
                ldln.go(lnp[:].rearrange("p a b -> p (a b)"), lnp_d[nxt].rearrange("p a b -> p (a b)"), w=["lnp"])

        def proj_fm(wt, wreg, sub, T):
            pb, preg = bank()
            for kc in range(NKC):
                k.op("pe", lambda e, kc=kc: e.matmul(pb[:, 0:T], lhsT=wt[:, kc, sub * 128:(sub + 1) * 128], rhs=hT[:, kc, 0:T],
                                                     start=(kc == 0), stop=(kc == NKC - 1)),
                     r=[wreg, ("hT", kc)], w=[preg])
            return pb, preg

        def gate2silu(pb, preg, out_ap, out_reg, tmp, tmp_reg, T):
            k.op("act", lambda e: e.activation(out=out_ap, in_=pb[:, 0:T], func=AF.Silu), r=[preg], w=[out_reg])

        st_out = [DmaSem(k, f"st_out{i}") for i in range(4)]
        dbgs = DmaSem(k, "dbgs") if dbg else None

        for (tkind, T, pos0, ridx) in tiles:
            is_meta = tkind == "meta"
            nsub = 1 if is_meta else 4
            P = NMETA if is_meta else 128
            nblk = 0 if is_meta else 4
            if is_meta:
                ldx[0].go(h[0:NMETA, 0, :], meta_d, w=[("h", 0)])
            layer_norm_rows(T, nsub, P, 0, staged=not is_meta)
            nxt_ridx = 0 if is_meta else ridx + 1
            ldpos.go(F[0][:, 0:T], pos_d[:, pos0:pos0 + T], w=["F0"])
            k.op("dve", lambda e: e.tensor_scalar(out=F[1][:, 0:T], in0=F[0][:, 0:T], scalar1=rotc[:, 0:1], scalar2=None, op0=ALU.mult),
                 r=["F0", "rotc"], w=["F1"])
            for which, dst, dreg in ((0, St, "St"), (1, Ct, "Ct")):
                shift = 0.0 if which == 0 else math.pi / 2
                k.op("dve", lambda e, shift=shift: e.tensor_scalar(out=F[2][:, 0:T], in0=F[1][:, 0:T], scalar1=shift, scalar2=None, op0=ALU.add),
                     r=["F1"], w=["F2"])
                k.op("dve", lambda e: e.tensor_scalar(out=F[3][:, 0:T], in0=F[2][:, 0:T], scalar1=1.0 / (2 * math.pi), scalar2=None, op0=ALU.mult),
                     r=["F2"], w=["F3"])
                k.op("dve", lambda e: e.tensor_copy(out=F[0][:, 0:T].bitcast(I32), in_=F[3][:, 0:T]), r=["F3"], w=["F0"])
                k.op("dve", lambda e: e.tensor_copy(out=F[3][:, 0:T], in_=F[0][:, 0:T].bitcast(I32)), r=["F0"], w=["F3"])
                k.op("dve", lambda e: e.scalar_tensor_tensor(out=F[2][:, 0:T], in0=F[3][:, 0:T], scalar=-6.28125, in1=F[2][:, 0:T],
                                                             op0=ALU.mult, op1=ALU.add), r=["F3", "F2"], w=["F2"])
                k.op("dve", lambda e: e.scalar_tensor_tensor(out=F[2][:, 0:T], in0=F[3][:, 0:T], scalar=-(2 * math.pi - 6.28125), in1=F[2][:, 0:T],
                                                             op0=ALU.mult, op1=ALU.add), r=["F3", "F2"], w=["F2"])
                k.op("dve", lambda e: e.tensor_scalar(out=F[2][:, 0:T], in0=F[2][:, 0:T], scalar1=-3.1415925, scalar2=3.1415925, op0=ALU.max, op1=ALU.min),
                     r=["F2"], w=["F2"])
                k.op("act", lambda e, dst=dst: e.activation(out=dst[:, 0:T], in_=F[2][:, 0:T], func=AF.Sin), r=["F2"], w=[dreg])
            k.op("dve", lambda e: e.tensor_scalar(out=St[:, 0:T], in0=St[:, 0:T], scalar1=rotc[:, 1:2], scalar2=None, op0=ALU.mult),
                 r=["St", "rotc"], w=["St"])
            if dbg and ridx == 0 and not is_meta:
                dbgs.go(dbg_d["d_h0"], h[:].rearrange("p a b -> p (a b)"), r=[("h", s) for s in range(4)])

            for l in range(2):
                ldsw.group()
                ldsw.go(pwb[:].rearrange("p a b -> p (a b)"), pwb_d[l], r=[("pwb_d", l)], w=["pwb"])
                ldsw.go(bdb[:].rearrange("p g c m -> p (g c m)"), bdb_d[l], r=[("bdb_d", l)], w=["bdb"])
                ldsw.end()
                k.cur_tag = "hT"
                ev = 0
                for s in range(nsub):
                    j = s % 2
                    hb = qT[:, 4 * j:4 * j + 4, :].rearrange("p a b -> p (a b)")
                    hbreg = [("qT", 4 * j + q) for q in range(4)]
                    if s % 2 == 0:
                        k.op("act", lambda e, s=s, hb=hb: e.activation(out=hb[0:P, :], in_=h[0:P, s, :], func=AF.Copy), r=[("h", s)], w=hbreg)
                    else:
                        k.op("dve", lambda e, s=s, hb=hb: e.tensor_copy(out=hb[0:P, :], in_=h[0:P, s, :]), r=[("h", s)], w=hbreg)
                    for k4 in range(4):
                        pb, preg = bank()
                        pbv = pb[:, 0:256].bitcast(BF16)
                        for kk in range(4):
                            kc = k4 * 4 + kk
                            k.op("pe", lambda e, kc=kc, kk=kk, pbv=pbv, hb=hb: e.transpose(pbv[:, kk * 128: kk * 128 + P], hb[0:P, kc * 128:(kc + 1) * 128], identb[0:P, 0:P]),
                                 r=hbreg + ["identb"], w=[preg])
                        src = pbv.rearrange("p (a b) -> p a b", a=4)[:, :, 0:P]
                        dst = hT[:, k4 * 4:(k4 + 1) * 4, s * 128:s * 128 + P]
                        regs = [("hT", k4 * 4 + kk) for kk in range(4)]
                        if ev % 2 == 0:
                            k.op("act", lambda e, src=src, dst=dst: e.activation(out=dst, in_=src, func=AF.Copy), r=[preg], w=regs)
                        else:
                            k.op("dve", lambda e, src=src, dst=dst: e.tensor_copy(out=dst, in_=src), r=[preg], w=regs)
                        ev += 1

                k.cur_tag = "conv1"
                for c in range(4):
                    k.op("act", lambda e, c=c: e.activation(out=cg[:, c, 0:30], in_=cv_halo[:, l, c, :], func=AF.Copy), r=[f"cv_halo{l}"], w=[("cg", c)])
                for gi in range(2):
                    wt, wreg = w_get(("in", l, 2 + gi))
                    for sub in range(2):
                        pb, preg = proj_fm(wt, wreg, sub, T)
                        tmp = F[sub]
                        k.op("act", lambda e, pb=pb, tmp=tmp: e.activation(out=tmp[:, 0:T], in_=pb[:, 0:T], func=AF.Tanh, scale=0.5),
                             r=[preg], w=[f"F{sub}"])
                    w_issue()
                    wt, wreg = w_get(("in", l, 0 + gi))
                    for sub in range(2):
                        c = gi * 2 + sub
                        pb, preg = proj_fm(wt, wreg, sub, T)
                        k.op("dve", lambda e, pb=pb, c=c, sub=sub: e.scalar_tensor_tensor(out=cg[:, c, 30:30 + T], in0=F[sub][:, 0:T], scalar=1.0,
                                                                                      in1=pb[:, 0:T], op0=ALU.add, op1=ALU.mult),
                             r=[f"F{sub}", preg], w=[("cg", c)])
                    w_issue()
                pm, pmreg = banks[6], "P6"
                pq, pqreg = banks[7], "P7"
                for c in range(4):
                    A, Areg = F[12 + c], f"F{12 + c}"
                    Bq, Breg = F[16 + c % 2], f"F{16 + c % 2}"
                    k.op("dve", lambda e, c=c, A=A: e.tensor_scalar(out=A[:, 0:T], in0=cg[:, c, 0:T], scalar1=pcs(l, c, 0, True), scalar2=pcs(l, c, 31),
                                                                    op0=ALU.mult, op1=ALU.add), r=[("cg", c), "pc", "pch"], w=[Areg])
                    for tap in range(1, 31):
                        k.op("dve", lambda e, c=c, A=A, tap=tap: e.scalar_tensor_tensor(out=A[:, 0:T], in0=cg[:, c, tap:tap + T], scalar=pcs(l, c, tap, True),
                                                                                    in1=A[:, 0:T], op0=ALU.mult, op1=ALU.add),
                             r=[("cg", c), "pch", Areg], w=[Areg])
                    k.op("act", lambda e, c=c: e.activation(out=cv_halo[:, l, c, :], in_=cg[:, c, T:T + 30], func=AF.Copy), r=[("cg", c)], w=[f"cv_halo{l}"])
                    k.op("act", lambda e, A=A, Bq=Bq: e.activation(out=Bq[:, 0:T], in_=A[:, 0:T], func=AF.Square), r=[Areg], w=[Breg])
                    k.op("pe", lambda e, c=c, A=A: e.matmul(pm[:, 0:T], lhsT=onesf[:, :], rhs=A[:, 0:T], start=(c == 0), stop=(c == 3)),
                         r=["onesf", Areg], w=[pmreg])
                    k.op("pe", lambda e, c=c, Bq=Bq: e.matmul(pq[:, 0:T], lhsT=onesf[:, :], rhs=Bq[:, 0:T], start=(c == 0), stop=(c == 3)),
                         r=["onesf", Breg], w=[pqreg])
                k.op("act", lambda e: e.activation(out=F[0][:, 0:T], in_=pm[:, 0:T], func=AF.Copy), r=[pmreg], w=["F0"])
                k.op("dve", lambda e: e.tensor_tensor(out=F[1][:, 0:T], in0=F[0][:, 0:T], in1=F[0][:, 0:T], op=ALU.mult), r=["F0"], w=["F1"])
                k.op("dve", lambda e: e.scalar_tensor_tensor(out=F[1][:, 0:T], in0=F[1][:, 0:T], scalar=-1.0, in1=pq[:, 0:T], op0=ALU.mult, op1=ALU.add),
                     r=["F1", pqreg], w=["F1"])
                k.op("dve", lambda e: e.tensor_scalar(out=F[1][:, 0:T], in0=F[1][:, 0:T], scalar1=LN_EPS, scalar2=None, op0=ALU.add), r=["F1"], w=["F1"])
                k.op("act", lambda e: e.activation(out=F[1][:, 0:T], in_=F[1][:, 0:T], func=AF.Sqrt), r=["F1"], w=["F1"])
                k.op("dve", lambda e: e.reciprocal(out=F[1][:, 0:T], in_=F[1][:, 0:T]), r=["F1"], w=["F1"])
                k.op("dve", lambda e: e.scalar_tensor_tensor(out=F[0][:, 0:T], in0=F[0][:, 0:T], scalar=-1.0, in1=F[1][:, 0:T], op0=ALU.mult, op1=ALU.mult),
                     r=["F0", "F1"], w=["F0"])
                for c in range(4):
                    A, Areg = F[12 + c], f"F{12 + c}"
                    Bq, Breg = F[16 + c % 2], f"F{16 + c % 2}"
                    k.op("dve", lambda e, A=A: e.tensor_tensor(out=A[:, 0:T], in0=A[:, 0:T], in1=F[1][:, 0:T], op=ALU.mult), r=[Areg, "F1"], w=[Areg])
                    k.op("dve", lambda e, A=A: e.tensor_tensor(out=A[:, 0:T], in0=A[:, 0:T], in1=F[0][:, 0:T], op=ALU.add), r=[Areg, "F0"], w=[Areg])
                    k.op("act", lambda e, A=A, Bq=Bq, c=c: e.activation(out=Bq[:, 0:T], in_=A[:, 0:T], func=AF.Tanh, scale=pcs(l, c, 32, True), bias=pcs(l, c, 33, True)),
                         r=[Areg, "pch"], w=[Breg])
                    k.op("dve", lambda e, A=A, c=c: e.tensor_scalar(out=A[:, 0:T], in0=A[:, 0:T], scalar1=pcs(l, c, 32, True), scalar2=pcs(l, c, 33, True), op0=ALU.mult, op1=ALU.add),
                         r=[Areg, "pch"], w=[Areg])
                    k.op("dve", lambda e, A=A, Bq=Bq, c=c: e.scalar_tensor_tensor(out=B[c][:, 0:T], in0=Bq[:, 0:T], scalar=1.0, in1=A[:, 0:T], op0=ALU.add, op1=ALU.mult),
                         r=[Areg, Breg], w=[f"B{c}"])

                k.cur_tag = "lru"
                for c in range(4):
                    k.op("act", lambda e, c=c: e.activation(out=lx[:, c, 0:3], in_=lc_halo[:, l, c, :], func=AF.Copy), r=[f"lc_halo{l}"], w=[("lx", c)])
                for gi in range(2):
                    wt, wreg = w_get(("in", l, 16 + gi))
                    for sub in range(2):
                        c = gi * 2 + sub
                        pb, preg = proj_fm(wt, wreg, sub, T)
                        k.op("act", lambda e, pb=pb, c=c: e.activation(out=lx[:, c, 3:3 + T], in_=pb[:, 0:T], func=AF.Copy), r=[preg], w=[("lx", c)])
                    w_issue()
                for gi in range(2):
                    wt, wreg = w_get(("in", l, 18 + gi))
                    for sub in range(2):
                        c = gi * 2 + sub
                        pb, preg = proj_fm(wt, wreg, sub, T)
                        gate2silu(pb, preg, F[18 + c][:, 0:T], f"F{18 + c}", F[2 + sub], f"F{2 + sub}", T)
                    w_issue()
                for c in range(4):
                    X, Xreg = F[12], "F12"
                    k.op("dve", lambda e, c=c: e.tensor_scalar(out=X[:, 0:T], in0=lx[:, c, 0:T], scalar1=pcs(l, c, 35), scalar2=pcs(l, c, 39), op0=ALU.mult, op1=ALU.add),
                         r=[("lx", c), "pc"], w=[Xreg])
                    for tap in range(1, 4):
                        k.op("dve", lambda e, c=c, tap=tap: e.scalar_tensor_tensor(out=X[:, 0:T], in0=lx[:, c, tap:tap + T], scalar=pcs(l, c, 35 + tap), in1=X[:, 0:T],
                                                                               op0=ALU.mult, op1=ALU.add), r=[("lx", c), "pc", Xreg], w=[Xreg])
                    k.op("act", lambda e, c=c: e.activation(out=lc_halo[:, l, c, :], in_=lx[:, c, T:T + 3], func=AF.Copy), r=[("lx", c)], w=[f"lc_halo{l}"])
                    k.op("act", lambda e: e.activation(out=B[4][:, 0:T], in_=X[:, 0:T], func=AF.Copy), r=[Xreg], w=["B4"])
                    pa, pareg = bank(aux=True)
                    px, pxreg = bank(aux=True)
                    k.op("pe", lambda e, c=c, pa=pa: e.matmul(pa[:, 0:T], lhsT=bdb[:, 0, c, :], rhs=B[4][:, 0:T], start=True, stop=True), r=["bdb", "B4"], w=[pareg])
                    k.op("pe", lambda e, c=c, px=px: e.matmul(px[:, 0:T], lhsT=bdb[:, 1, c, :], rhs=B[4][:, 0:T], start=True, stop=True), r=["bdb", "B4"], w=[pxreg])
                    k.op("act", lambda e, c=c, pa=pa: e.activation(out=F[13][:, 0:T], in_=pa[:, 0:T], func=AF.Tanh, scale=0.5, bias=pcs(l, c, 40, True)), r=[pareg, "pch"], w=["F13"])
                    k.op("act", lambda e, c=c, px=px: e.activation(out=F[14][:, 0:T], in_=px[:, 0:T], func=AF.Tanh, scale=0.5, bias=pcs(l, c, 41, True)), r=[pxreg, "pch"], w=["F14"])
                    hci = l * 4 + c
                    k.op("act", lambda e, hci=hci: e.activation(out=F[15][:, 0:T], in_=F[13][:, 0:T], func=AF.Exp, scale=hc[:, hci:hci + 1], bias=hc[:, hci:hci + 1]), r=["F13", "hc"], w=["F15"])
                    k.op("act", lambda e, hci=hci: e.activation(out=F[16][:, 0:T], in_=F[13][:, 0:T], func=AF.Tanh, scale=nhc[:, hci:hci + 1], bias=nhc[:, hci:hci + 1]), r=["F13", "nhc"], w=["F16"])
                    k.op("act", lambda e: e.activation(out=F[17][:, 0:T], in_=F[15][:, 0:T], func=AF.Square), r=["F15"], w=["F17"])
                    k.op("dve", lambda e: e.scalar_tensor_tensor(out=F[16][:, 0:T], in0=F[17][:, 0:T], scalar=1.0, in1=F[16][:, 0:T], op0=ALU.add, op1=ALU.mult), r=["F17", "F16"], w=["F16"])
                    k.op("act", lambda e: e.activation(out=F[16][:, 0:T], in_=F[16][:, 0:T], func=AF.Sqrt, scale=0.25), r=["F16"], w=["F16"])
                    k.op("dve", lambda e: e.scalar_tensor_tensor(out=F[14][:, 0:T], in0=F[14][:, 0:T], scalar=1.0, in1=X[:, 0:T], op0=ALU.add, op1=ALU.mult), r=["F14", Xreg], w=["F14"])
                    k.op("dve", lambda e: e.tensor_tensor(out=F[14][:, 0:T], in0=F[14][:, 0:T], in1=F[16][:, 0:T], op=ALU.mult), r=["F14", "F16"], w=["F14"])
                    k.op("dve", lambda e, c=c: e.tensor_tensor_scan(out=F[17][:, 0:T], data0=F[15][:, 0:T], data1=F[14][:, 0:T], initial=lru_st[:, l, c:c + 1],
                                                                   op0=ALU.mult, op1=ALU.add), r=["F15", "F14", f"lru_st{l}"], w=["F17"])
                    k.op("dve", lambda e, c=c: e.tensor_copy(out=lru_st[:, l, c:c + 1], in_=F[17][:, T - 1:T]), r=["F17"], w=[f"lru_st{l}"])
                    k.op("dve", lambda e, c=c: e.scalar_tensor_tensor(out=yT[:, 12 + c, 0:T], in0=F[17][:, 0:T], scalar=1.0, in1=F[18 + c][:, 0:T], op0=ALU.mult, op1=ALU.mult),
                         r=["F17", f"F{18 + c}"], w=[("yT", 12 + c)])

                k.cur_tag = "qkv"
                if not is_meta and ridx > 0:
                    k.op("act", lambda e: e.activation(out=kT[:, :, 0:128], in_=kprev[:, l, :, :], func=AF.Copy), r=[f"kprev{l}"], w=["kT"])

                rot_list = []

                def rot_tile(pb, preg, out_ap, out_reg):
                    k.op("act", lambda e: e.activation(out=out_ap, in_=pb[:, 0:T], func=AF.Copy), r=[preg], w=[out_reg])
                    rot_list.append((out_ap, out_reg))

                def rot_pass():
                    for i, (ap, reg) in enumerate(rot_list):
                        fa, fareg = (F[2], "F2")
                        fb, fbreg = (F[3], "F3")
                        p2, p2reg = bank(aux=True)
                        k.op("pe", lambda e, ap=ap, p2=p2: e.matmul(p2[:, 0:T], lhsT=permb[:, :], rhs=ap, start=True, stop=True), r=["permb", reg], w=[p2reg])
                        k.op("dve", lambda e, p2=p2: e.tensor_tensor(out=fb[:, 0:T], in0=p2[:, 0:T], in1=St[:, 0:T], op=ALU.mult), r=[p2reg, "St"], w=[fbreg])
                        k.op("dve", lambda e, ap=ap: e.tensor_tensor(out=fa[:, 0:T], in0=ap, in1=Ct[:, 0:T], op=ALU.mult), r=[reg, "Ct"], w=[fareg])
                        k.op("dve", lambda e, ap=ap: e.tensor_tensor(out=ap, in0=fa[:, 0:T], in1=fb[:, 0:T], op=ALU.add), r=[fareg, fbreg], w=[reg])

                for gi in range(4):
                    wt, wreg = w_get(("in", l, 6 + gi))
                    for sub in range(2):
                        i = gi * 2 + sub
                        pb, preg = proj_fm(wt, wreg, sub, T)
                        rot_tile(pb, preg, qT[:, i, 0:T], ("qT", i))
                    w_issue()
                wt, wreg = w_get(("in", l, 10))
                for sub in range(2):
                    pb, preg = proj_fm(wt, wreg, sub, T)
                    if is_meta:
                        rot_tile(pb, preg, kmeta[:, l, sub, 0:T], f"kmeta{l}")
                    else:
                        rot_tile(pb, preg, kT[:, sub, 128:128 + T], "kT")
                w_issue()
                wt, wreg = w_get(("in", l, 11))
                for b in range(max(nblk, 1)):
                    pb, preg = bank()
                    for kc in range(NKC):
                        k.op("pe", lambda e, kc=kc, b=b, pb=pb: e.matmul(pb[0:P, 0:256], lhsT=hT[:, kc, b * 128:b * 128 + P], rhs=wt[:, kc, 0:256],
                                                                     start=(kc == 0), stop=(kc == NKC - 1)), r=[wreg, ("hT", kc)], w=[preg])
                    for j in range(4):
                        off = 0 if j % 2 == 0 else 64
                        if is_meta:
                            k.op("act", lambda e, j=j, off=off, pb=pb: e.activation(out=vmeta[0:P, l, j, off:off + 64], in_=pb[0:P, j * 64:(j + 1) * 64], func=AF.Copy),
                                 r=[preg], w=[f"vmeta{l}"])
                        else:
                            k.op("act", lambda e, j=j, off=off, pb=pb, b=b: e.activation(out=vaug[:, b, j, off:off + 64], in_=pb[:, j * 64:(j + 1) * 64], func=AF.Copy),
                                 r=[preg], w=["vaug"])
                w_issue()
                k.cur_tag = "gates"
                for gi in range(4):
                    wt, wreg = w_get(("in", l, 12 + gi))
                    for sub in range(2):
                        i = gi * 2 + sub
                        pb, preg = proj_fm(wt, wreg, sub, T)
                        gate2silu(pb, preg, sga[:, i, 0:T], ("sga", i), None, None, T)
                    w_issue()
                k.cur_tag = "conv2"
                for gi in range(2):
                    wt, wreg = w_get(("in", l, 4 + gi))
                    for sub in range(2):
                        co = gi * 2 + sub
                        pg, pgreg = proj_fm(wt, wreg, sub, T)
                        G, Greg = F[sub], f"F{sub}"
                        gate2silu(pg, pgreg, G[:, 0:T], Greg, F[2 + sub], f"F{2 + sub}", T)
                        pb, preg = bank(aux=True)
                        for ci in range(4):
                            k.op("pe", lambda e, ci=ci, co=co, pb=pb: e.matmul(pb[:, 0:T], lhsT=pwb[:, ci, co * 128:(co + 1) * 128], rhs=B[ci][:, 0:T],
                                                                          start=(ci == 0), stop=(ci == 3)), r=["pwb", f"B{ci}"], w=[preg])
                        k.op("dve", lambda e, co=co, pb=pb, G=G: e.scalar_tensor_tensor(out=yT[:, co, 0:T], in0=pb[:, 0:T], scalar=pcs(l, co, 34), in1=G[:, 0:T],
                                                                              op0=ALU.add, op1=ALU.mult), r=[preg, "pc", Greg], w=[("yT", co)])
                    w_issue()

                if l == 1 and nxt_ridx < n_real:
                    for s in (0, 1):
                        ldx[s].go(xs[s], x_d[nxt_ridx * TT + s * 128: nxt_ridx * TT + (s + 1) * 128, :], w=xs_reg[s])
                rot_pass()
                k.cur_tag = "units"
                nq = NMETA if is_meta else 128
                for b in range(max(nblk, 1)):
                    for j in range(4):
                        half = j % 2
                        hs = slice(half * 64, half * 64 + 64)
                        ds = slice((1 - half) * 64, (1 - half) * 64 + 64)
                        t0 = 4 * (j // 2)
                        q_rhs = qT[hs, t0:t0 + 4, b * 128:b * 128 + nq]
                        NQ = 4 * nq
                        kblocks = []
                        if is_meta:
                            kblocks.append((kmeta[hs, l, j // 2, 0:NMETA], NMETA, maskm[0:NMETA, 0:64], vmeta[0:NMETA, l, j, :], [f"kmeta{l}", f"vmeta{l}", "maskm"]))
                        else:
                            kblocks.append((kmeta[hs, l, j // 2, 0:NMETA], NMETA, None, vmeta[0:NMETA, l, j, :], [f"kmeta{l}", f"vmeta{l}"]))
                            if not (ridx == 0 and b == 0):
                                kblocks.append((kT[hs, j // 2, b * 128:(b + 1) * 128], 128, maskb[:, 0:1, :].to_broadcast([128, 4, 128]), (vprev[:, l, j, :] if b == 0 else vaug[:, b - 1, j, :]), ["kT", "vaug", f"vprev{l}", "maskb"]))
                            kblocks.append((kT[hs, j // 2, (b + 1) * 128:(b + 2) * 128], 128, maskb[:, 1:2, :].to_broadcast([128, 4, 128]), vaug[:, b, j, :], ["kT", "vaug", "maskb"]))
                        ex = []
                        for bi, (k_l, nk, m_ap, v_l, regs) in enumerate(kblocks):
                            pb, preg = bank()
                            k.op("pe", lambda e, pb=pb, k_l=k_l, nk=nk, m_ap=m_ap: e.matmul(pb[0:nk, 0:NQ], lhsT=k_l, rhs=q_rhs, start=True, stop=(m_ap is None)),
                                 r=regs + [("qT", t0 + g) for g in range(4)], w=[preg])
                            if m_ap is not None:
                                k.op("pe", lambda e, pb=pb, nk=nk, m_ap=m_ap: e.matmul(pb[0:nk, 0:NQ], lhsT=identb[0:nk, 0:nk], rhs=m_ap, start=False, stop=True),
                                     r=regs + ["identb"], w=[preg])
                            ebi = (unit_i[0] % 2) * 3 + bi
                            eb = B[ebi]
                            k.op("act", lambda e, pb=pb, nk=nk, eb=eb: e.activation(out=eb[0:nk, 0:NQ], in_=pb[0:nk, 0:NQ], func=AF.Exp, scale=0.125), r=[preg], w=[f"B{ebi}"])
                            ex.append((eb, f"B{ebi}", nk, v_l, regs))
                        unit_i[0] += 1
                        po, poreg = bank(aux=True)
                        for bi, (eb, ereg, nk, v_l, regs) in enumerate(ex):
                            k.op("pe", lambda e, eb=eb, nk=nk, v_l=v_l, bi=bi: e.matmul(po[:, 0:NQ], lhsT=v_l, rhs=eb[0:nk, 0:NQ], start=(bi == 0), stop=False),
                                 r=regs + [ereg], w=[poreg])
                        sp_ap = sinkp[0:1, (l * 4 + j) * 4:(l * 4 + j) * 4 + 4].rearrange("p (g o) -> p g o", o=1).to_broadcast([1, 4, nq])
                        k.op("pe", lambda e: e.matmul(po[:, 0:NQ], lhsT=srow[0:1, half * 128:(half + 1) * 128], rhs=sp_ap, start=False, stop=True),
                             r=["srow", "sinkp"], w=[poreg])
                        k.op("act", lambda e: e.activation(out=F[2][ds, 0:NQ], in_=po[ds, 0:NQ], func=AF.Ln), r=[poreg], w=["F2"])
                        k.op("act", lambda e: e.activation(out=F[0][ds, 0:NQ], in_=F[2][ds, 0:NQ], func=AF.Exp, scale=-1.0), r=["F2"], w=["F0"])
                        k.op("dve", lambda e: e.tensor_tensor(out=F[1][hs, 0:NQ], in0=po[hs, 0:NQ], in1=F[0][ds, 0:NQ], op=ALU.mult), r=[poreg, "F0"], w=["F1"])
                        k.op("dve", lambda e: e.tensor_tensor(out=yT[hs, 4 + t0:4 + t0 + 4, b * 128:b * 128 + nq],
                                                               in0=F[1][hs, 0:NQ].rearrange("p (g q) -> p g q", g=4),
                                                               in1=sga[hs, t0:t0 + 4, b * 128:b * 128 + nq], op=ALU.mult),
                             r=["F1"] + [("sga", t0 + g) for g in range(4)], w=[("yT", 4 + t0 + g) for g in range(4)])
                if not is_meta:
                    k.op("act", lambda e: e.activation(out=kprev[:, l, :, :], in_=kT[:, :, TT:TT + 128], func=AF.Copy), r=["kT"], w=[f"kprev{l}"])
                    k.op("act", lambda e: e.activation(out=vprev[:, l, :, :], in_=vaug[:, 3, :, :], func=AF.Copy), r=["vaug"], w=[f"vprev{l}"])
                if dbg and ridx == 0 and not is_meta and l == 0:
                    dbgs.go(dbg_d["d_yT"], yT[:].rearrange("p a b -> p (a b)"), r=[("yT", i) for i in range(16)])

                k.cur_tag = "wout"
                for g in range(8):
                    wt, wreg = w_get(("out", l, g))
                    for s in range(nsub):
                        pb, preg = bank()
                        for kc in range(NKC):
                            k.op("pe", lambda e, kc=kc, s=s, pb=pb: e.matmul(pb[0:P, 0:256], lhsT=yT[:, kc, s * 128:s * 128 + P], rhs=wt[:, kc, 0:256],
                                                                         start=(kc == 0), stop=(kc == NKC - 1)), r=[wreg, ("yT", kc)], w=[preg])
                        k.op("dve", lambda e, s=s, g=g, pb=pb: e.scalar_tensor_tensor(out=h[0:P, s, g * 256:(g + 1) * 256], in0=h[0:P, s, g * 256:(g + 1) * 256], scalar=ALPHA,
                                                                                 in1=pb[0:P, 0:256], op0=ALU.mult, op1=ALU.add), r=[("h", s), preg], w=[("h", s)])
                        if g % 2 == 1:
                            c4 = g // 2
                            k.op("dve", lambda e, s=s, c4=c4: e.bn_stats(out=stat[0:P, s, c4 * 6:(c4 + 1) * 6], in_=h[0:P, s, c4 * 512:(c4 + 1) * 512]),
                                 r=[("h", s)], w=[("stat", s, c4)])
                    w_issue()
                if l == 1 and nxt_ridx < n_real:
                    for s in (2, 3):
                        ldx[s].go(xs[s], x_d[nxt_ridx * TT + s * 128: nxt_ridx * TT + (s + 1) * 128, :], w=xs_reg[s])
                layer_norm_rows(T, nsub, P, 1 + l, stats_done=True)
                if dbg and ridx == 0 and not is_meta and l == 0:
                    dbgs.go(dbg_d["d_h1"], h[:].rearrange("p a b -> p (a b)"), r=[("h", s) for s in range(4)])
            if not is_meta:
                for s in range(4):
                    st_out[s].go(y_d[ridx * TT + s * 128: ridx * TT + (s + 1) * 128, :], h[:, s, :], r=[("h", s)])
        for s in range(4):
            k.wait_final("sp", st_out[s].sem, 16 * st_out[s].n)
        if dbg:
            k.wait_final("sp", dbgs.sem, 16 * dbgs.n)
        k.finish()
    return nc


def _host_consts(L):
    ident = np.eye(128, dtype=np.float32)
    perm = np.zeros((128, 128), np.float32)
    for m in range(128):
        d = m % 64
        if d < 8:
            perm[m + 8, m] = 1.0
        elif d < 16:
            perm[m - 8, m] = 1.0
    kj = np.arange(128)[:, None]
    qi = np.arange(128)[None, :]
    mprev = np.where(kj > qi, 0.0, MASKV).astype(np.float32)
    mcur = np.where(kj <= qi, 0.0, MASKV).astype(np.float32)
    maskb = np.stack([mprev, mcur], axis=1).reshape(128, 256)
    km = np.arange(16)[:, None]
    qm = np.arange(16)[None, :]
    maskm = np.tile(np.where(km <= qm, 0.0, MASKV).astype(np.float32), (1, 4))
    pos = np.tile(np.arange(L, dtype=np.float32)[None, :], (128, 1))
    half = 8
    inv_freq = (np.float32(500000.0) ** (-np.arange(half, dtype=np.float32) / np.float32(half))).astype(np.float32)
    rotc = np.zeros((128, 2), np.float32)
    for p in range(128):
        d = p % 64
        if d < 16:
            rotc[p, 0] = inv_freq[d % 8]
            rotc[p, 1] = -1.0 if d < 8 else 1.0
    sinkrow = np.zeros((1, 256), np.float32)
    sinkrow[0, 64:128] = 1.0
    sinkrow[0, 128:192] = 1.0
    return dict(ident=ident, perm=perm, maskb=np.ascontiguousarray(maskb), maskm=np.ascontiguousarray(maskm), pos=pos, rotc=rotc, sinkrow=sinkrow)


def _layout_params(p, n_real):
    f = lambda a: np.ascontiguousarray(np.asarray(a, dtype=np.float32))
    ap = attn_perm()
    w_in = f(p["w_in"]).copy()
    w_in[:, :, 1536:2560] = w_in[:, :, 1536:2560][:, :, ap]
    w_in[:, :, 3072:4096] = w_in[:, :, 3072:4096][:, :, ap]
    w_in_r = np.ascontiguousarray(w_in.reshape(2, NKC, 128, 20, 256).transpose(0, 3, 2, 1, 4)).reshape(2, 20, 128, NKC * 256)
    w_out = f(p["w_out"]).copy()
    w_out[:, 512:1536, :] = w_out[:, 512:1536, :][:, ap, :]
    w_out_r = np.ascontiguousarray(w_out.reshape(2, NKC, 128, 8, 256).transpose(0, 3, 2, 1, 4)).reshape(2, 8, 128, NKC * 256)
    pw_r = np.ascontiguousarray(f(p["conv_pw_w"]).reshape(2, 4, 128, 512).transpose(0, 2, 1, 3)).reshape(2, 128, 2048)
    bd = np.zeros((2, 128, 2, 4, 128), np.float32)
    for gi, nm in enumerate(("lru_wa", "lru_wx")):
        w = f(p[nm])
        for c in range(4):
            for hh in range(2):
                bd[:, hh * 64:(hh + 1) * 64, gi, c, hh * 64:(hh + 1) * 64] = w[:, 2 * c + hh]
    lru_bd = bd.reshape(2, 128, 1024)
    pc = np.zeros((128, 2, 4, NPCL), np.float32)

    def chan(a):
        return f(a).reshape(2, 4, 128).transpose(2, 0, 1)
    pc[:, :, :, 0:31] = f(p["conv_dw_w"]).reshape(2, 31, 4, 128).transpose(3, 0, 2, 1)
    pc[:, :, :, 31] = chan(p["conv_dw_b"])
    pc[:, :, :, 32] = chan(p["conv_ln_g"])
    pc[:, :, :, 33] = chan(p["conv_ln_b"])
    pc[:, :, :, 34] = chan(p["conv_pw_b"])
    pc[:, :, :, 35:39] = f(p["lru_conv_w"]).reshape(2, 4, 4, 128).transpose(3, 0, 2, 1)
    pc[:, :, :, 39] = chan(p["lru_conv_b"])
    pc[:, :, :, 40] = chan(p["lru_ba"])
    pc[:, :, :, 41] = chan(p["lru_bx"])
    pc[:, :, :, 42] = chan(p["lru_lambda"])
    pc = pc.reshape(128, 2 * 4 * NPCL)
    sinks = f(p["attn_sinks"])
    sk = np.zeros((2, 4, 4), np.float32)
    for l in range(2):
        for j in range(4):
            for g in range(4):
                sk[l, j, g] = sinks[l, head_of(4 * (j // 2) + g, j % 2)]
    sinks_rep = sk.reshape(1, 32)
    lnp = np.zeros((3, 128, 2, D), np.float32)
    lnp[0, :, 0, :] = f(p["ln_in_g"])[None]
    lnp[0, :, 1, :] = f(p["ln_in_b"])[None]
    for l in range(2):
        lnp[1 + l, :, 0, :] = f(p["ln_post_g"])[l][None]
        lnp[1 + l, :, 1, :] = f(p["ln_post_b"])[l][None]
    out = dict(meta=f(p["meta_tokens"]), lnp=lnp, w_in_r=w_in_r, w_out_r=w_out_r, pw_r=pw_r, lru_bd=lru_bd, pc=pc, sinks_rep=sinks_rep)
    out.update(_host_consts(NMETA + n_real * TT))
    return out


def run(inputs, n_real=8, dbg=False):
    x = np.asarray(inputs["x"], dtype=np.float32)
    nb = x.shape[0]
    shared = _layout_params(inputs, n_real)
    nc = build(n_real=n_real, dbg=dbg)
    in_maps = []
    for b in range(nb):
        m = dict(shared)
        m["x"] = np.ascontiguousarray(x[b, :n_real * TT])
        in_maps.append(m)
    res = run_bass_kernel_spmd(nc, in_maps, core_ids=list(range(nb)))
    return res


def kernel(**inputs):
    res = run(inputs, n_real=SEQ // TT)
    return np.stack([np.asarray(r["y"], dtype=np.float32) for r in res.results], axis=0)
```

```python
import math
from contextlib import ExitStack

import numpy as np
import concourse.bass as bass
import concourse.mybir as mybir
from concourse.bass_utils import run_bass_kernel_spmd

F32 = mybir.dt.float32
BF16 = mybir.dt.bfloat16
I32 = mybir.dt.int32
AF = mybir.ActivationFunctionType
ALU = mybir.AluOpType

D = 2048
NKC = 16
NMETA = 16
TT = 512
SEQ = 4096
DEPTH = 2
ALPHA = (2.0 * DEPTH) ** 0.25
LN_EPS = 1e-5
GROUP_ORDER = [2, 0, 3, 1, 16, 17, 18, 19, 6, 7, 8, 9, 10, 11, 12, 13, 14, 15, 4, 5]
NPCL = 44
MASKV = -30000.0
SEM_LIMIT = 4000


class _Rec:
    def __init__(self):
        self.calls = []

    def __getattr__(self, name):
        def f(*a, **kw):
            self.calls.append((name, a, kw))
            return self
        return f


def _free_size(ap):
    n = 1
    for d in list(ap.shape)[1:]:
        n *= int(d)
    return n


def _est_cost(e, call):
    name, a, kw = call
    out = kw.get("out", a[0] if a else None)
    n = _free_size(out) if out is not None else 64
    if e == "pe":
        c = max(n, 64) / 2.3 + 12.0
        lhs = kw.get("lhsT", None)
        if (name == "transpose" and a[1].dtype == F32) or (lhs is not None and lhs.dtype == F32):
            c *= 4.0
        return c
    if e == "dve":
        c = (n + 58) / 0.87
        if name == "reciprocal":
            c *= 5.0
        if name == "tensor_tensor_scan":
            c *= 2.0
        return c
    if e == "act":
        return (n + 110) / 1.2
    if e == "pool":
        return n * 1.73 + 450.0
    return 50.0


class K:
    def __init__(self, nc):
        self.nc = nc
        self.es = ExitStack()
        self.eng = {"pe": nc.tensor, "act": nc.scalar, "dve": nc.vector, "pool": nc.gpsimd, "sp": nc.sync}
        self.ops = []
        self.last_w = {}
        self.readers = {}
        self.sems = {e: [] for e in self.eng}

    def sb(self, name, shape, dt):
        return self.es.enter_context(self.nc.sbuf_tensor("sb_" + name, list(shape), dt))

    def ps(self, name, shape, dt):
        return self.es.enter_context(self.nc.psum_tensor("ps_" + name, list(shape), dt))

    def sem(self, name):
        return self.es.enter_context(self.nc.semaphore(name))

    def _record(self, op, r, w, pri=None):
        idx = len(self.ops)
        deps = {}
        for reg in r:
            t = self.last_w.get(reg)
            if t is not None:
                deps[t] = "raw"
        for reg in w:
            t = self.last_w.get(reg)
            if t is not None:
                deps.setdefault(t, "waw")
            for t in self.readers.get(reg, ()):
                deps.setdefault(t, "war")
        op["deps"] = deps
        op["pri"] = pri if pri is not None else (0 if any(isinstance(x, str) and x[0] == "P" and x[1:].isdigit() for x in r) else 1)
        self.ops.append(op)
        for reg in w:
            self.last_w[reg] = idx
            self.readers[reg] = []
        for reg in r:
            self.readers.setdefault(reg, []).append(idx)
        return idx

    def op(self, e, emit, r=(), w=(), cost=None, pri=None):
        rec = _Rec()
        emit(rec)
        assert len(rec.calls) == 1
        return self._record(dict(e=e, call=rec.calls[0], dma=None, cost=cost if cost is not None else _est_cost(e, rec.calls[0]), tag=getattr(self, "cur_tag", "")), r, w, pri=pri)

    def dma(self, out, in_, sem, grp, r=(), w=(), q="sp"):
        nbytes = _free_size(out) * int(out.shape[0]) * (2 if out.dtype == BF16 else 4)
        return self._record(dict(e=q, call=("dma_start", (), dict(out=out, in_=in_)), dma=(sem, grp), cost=2000.0 + nbytes / 120.0), r, w)

    def wait_final(self, q, sem, val):
        self._record(dict(e=q, call=None, dma=None, cost=10.0, final=(sem, val)), (), ())
        self.ops[-1]["deps"] = {i: "raw" for i, o in enumerate(self.ops[:-1]) if o["dma"] is not None and o["dma"][0] is sem}

    def finish(self, window=256):
        ops = self.ops
        n = len(ops)
        engines = list(self.eng)
        pending = {e: [i for i in range(n) if ops[i]["e"] == e] for e in engines}
        head = {e: 0 for e in engines}
        done = [False] * n
        fin = [0.0] * n
        free = {e: 0.0 for e in engines}
        order = {e: [] for e in engines}
        remaining = n
        while remaining:
            best = None
            for e in engines:
                lst = pending[e]
                hd = head[e]
                while hd < len(lst) and done[lst[hd]]:
                    hd += 1
                head[e] = hd
                if hd >= len(lst):
                    continue
                win = 1 if e == "sp" else window
                cnt = 0
                p = hd
                cand = None
                while p < len(lst) and cnt < win:
                    i = lst[p]
                    p += 1
                    if done[i]:
                        continue
                    cnt += 1
                    ok = True
                    st = free[e]
                    for j in ops[i]["deps"]:
                        if not done[j]:
                            ok = False
                            break
                        if fin[j] > st:
                            st = fin[j]
                    if not ok:
                        continue
                    key = (st if st > free[e] else free[e], ops[i]["pri"], i)
                    if cand is None or key < cand:
                        cand = key
                    if st <= free[e] + 1e-9 and ops[i]["pri"] == 0:
                        break
                if cand is not None and (best is None or (cand[0], cand[2]) < (best[0], best[1])):
                    best = (cand[0], cand[2], e)
            assert best is not None, "scheduler deadlock"
            st, i, e = best
            o = ops[i]
            done[i] = True
            remaining -= 1
            if o["dma"] is not None:
                free[e] = st + 60.0
                fin[i] = st + o["cost"]
            else:
                free[e] = st + o["cost"]
                fin[i] = free[e] + 60.0
            order[e].append(i)
            if getattr(self, "trace", None) is not None:
                self.trace.append((e, st, o["cost"], i, (o["call"][0] if o["call"] else "final") + ":" + o.get("tag", "")))
        self.est_ns = max(fin) if n else 0.0
        busy = {e: sum(ops[i]["cost"] for i in order[e] if ops[i]["dma"] is None) for e in engines}
        print("[sched] est total us %.1f" % (self.est_ns / 1e3), {e: round(b / 1e3, 1) for e, b in busy.items()}, "n_ops", n, flush=True)
        seqno = {}
        for e in engines:
            s = 0
            for i in order[e]:
                if ops[i]["dma"] is None and ops[i]["call"] is not None:
                    seqno[i] = s
                    s += 1
        for e in engines:
            ngen = (sum(1 for i in order[e] if i in seqno) + SEM_LIMIT - 1) // SEM_LIMIT
            self.sems[e] = [self.sem(f"s_{e}_{g}") for g in range(ngen)]
        for e in engines:
            eng = self.eng[e]
            waited = {}
            for i in order[e]:
                o = ops[i]
                for j, kind in o["deps"].items():
                    d = ops[j]
                    if d["dma"] is None and d["call"] is None:
                        continue
                    if d["dma"] is None and d["e"] == e:
                        if e == "pe":
                            continue
                    if d["dma"] is not None:
                        sem, grp = d["dma"]
                        val = grp[0]
                        assert val is not None
                        key = ("dma", sem.num)
                    else:
                        sq = seqno[j]
                        key = ("eng", d["e"], sq // SEM_LIMIT)
                        sem = self.sems[d["e"]][sq // SEM_LIMIT]
                        val = sq % SEM_LIMIT + 1
                        skip = False
                        for g2 in range(sq // SEM_LIMIT + 1, len(self.sems[d["e"]])):
                            if ("eng", d["e"], g2) in waited:
                                skip = True
                        if skip:
                            continue
                    if waited.get(key, -1) >= val:
                        continue
                    waited[key] = val
                    eng.wait_ge(sem, val)
                if o.get("final") is not None:
                    sem, val = o["final"]
                    eng.wait_ge(sem, val)
                    continue
                name, a, kw = o["call"]
                ins = getattr(eng, name)(*a, **kw)
                if o["dma"] is not None:
                    ins.then_inc(o["dma"][0], 16)
                else:
                    sq = seqno[i]
                    ins.then_inc(self.sems[e][sq // SEM_LIMIT], 1)


class DmaSem:
    def __init__(self, k, name):
        self.k = k
        self.sem = k.sem(name)
        self.n = 0
        self.grp = None

    def group(self):
        self.grp = [None]

    def end(self):
        self.grp[0] = 16 * self.n
        self.grp = None

    def go(self, out, in_, r=(), w=(), q="sp"):
        self.n += 1
        g = self.grp if self.grp is not None else [16 * self.n]
        return self.k.dma(out, in_, self.sem, g, r=r, w=w, q=q)

    def final(self):
        return ("dma", self.sem, [16 * self.n])


def head_of(tile, half):
    if tile < 4:
        return tile if half == 0 else 4 + tile
    return 8 + (tile - 4) if half == 0 else 12 + (tile - 4)


def attn_perm():
    idx = []
    for t in range(8):
        for hh in range(2):
            h = head_of(t, hh)
            idx.extend(range(h * 64, h * 64 + 64))
    return np.array(idx)


def pc_index(l, c, j):
    return (l * 4 + c) * NPCL + j


def build(n_real=8, dbg=False):
    S = n_real * TT
    L = NMETA + S
    nc = bass.Bass("TRN2", target_bir_lowering=False)

    def din(name, shape, dt=F32):
        return nc.dram_tensor(name, list(shape), dt, kind="ExternalInput").ap()

    x_d = din("x", [S, D])
    meta_d = din("meta", [NMETA, D])
    lnp_d = din("lnp", [3, 128, 2, D])
    win_d = din("w_in_r", [2, 20, 128, NKC * 256])
    wout_d = din("w_out_r", [2, 8, 128, NKC * 256])
    pw_d = din("pw_r", [2, 128, 4 * 512])
    bd_d = din("lru_bd", [2, 128, 2 * 4 * 128])
    pc_d = din("pc", [128, 2 * 4 * NPCL])
    sk_d = din("sinks_rep", [1, 32])
    ident_d = din("ident", [128, 128])
    perm_d = din("perm", [128, 128])
    maskb_d = din("maskb", [128, 2 * 128])
    maskm_d = din("maskm", [16, 64])
    pos_d = din("pos", [128, L])
    rc_d = din("rotc", [128, 2])
    srow_d = din("sinkrow", [1, 2 * 128])
    y_d = nc.dram_tensor("y", [S, D], F32, kind="ExternalOutput").ap()
    winb_d = nc.dram_tensor("w_in_b", [2, 20, 128, NKC * 256], BF16, kind="Internal").ap()
    woutb_d = nc.dram_tensor("w_out_b", [2, 8, 128, NKC * 256], BF16, kind="Internal").ap()
    pwb_d = nc.dram_tensor("pw_b16", [2, 128, 4 * 512], BF16, kind="Internal").ap()
    bdb_d = nc.dram_tensor("bd_b16", [2, 128, 2 * 4 * 128], BF16, kind="Internal").ap()
    dbg_d = {}
    if dbg:
        for nm, shp in (("d_h0", [128, 4 * D]), ("d_yT", [128, NKC * TT]), ("d_h1", [128, 4 * D])):
            dbg_d[nm] = nc.dram_tensor(nm, shp, F32 if nm != "d_yT" else BF16, kind="ExternalOutput").ap()

    k = K(nc)
    with k.es:
        h = k.sb("h", [128, 4, D], F32)
        hT = k.sb("hT", [128, NKC, TT], BF16)
        yT = k.sb("yT", [128, NKC, TT], BF16)
        _hTf = hT[:].rearrange("p a b -> p (a b)").bitcast(F32)
        _yTf = yT[:].rearrange("p a b -> p (a b)").bitcast(F32)
        xs = [_hTf[:, 0:D], _hTf[:, D:2 * D], _yTf[:, 0:D], _yTf[:, D:2 * D]]
        xs_reg = [[("hT", kc) for kc in range(0, 8)], [("hT", kc) for kc in range(8, 16)],
                  [("yT", kc) for kc in range(0, 8)], [("yT", kc) for kc in range(8, 16)]]
        NSLOT = 3
        wbuf = [k.sb(f"wbuf{i}", [128, NKC, 256], BF16) for i in range(NSLOT)]
        lnp = k.sb("lnp", [128, 2, D], F32)
        ident = k.sb("ident", [128, 128], F32)
        identb = k.sb("identb", [128, 128], BF16)
        permb = k.sb("permb", [128, 128], BF16)
        onesf = k.sb("onesf", [128, 128], F32)
        maskb = k.sb("maskb", [128, 2, 128], BF16)
        maskm = k.sb("maskm", [16, 64], BF16)
        pc = k.sb("pc", [128, 2 * 4 * NPCL], F32)
        pch = k.sb("pch", [128, 2 * 4 * NPCL], F32)
        hc = k.sb("hc", [128, 8], F32)
        nhc = k.sb("nhc", [128, 8], F32)
        rotc = k.sb("rotc", [128, 2], F32)
        sinkp = k.sb("sinkp", [1, 32], BF16)
        srow = k.sb("srow", [1, 2 * 128], BF16)
        pwb = k.sb("pwb", [128, 4, 512], BF16)
        bdb = k.sb("bdb", [128, 2, 4, 128], BF16)
        Ct = k.sb("Ct", [128, TT], F32)
        St = k.sb("St", [128, TT], F32)
        cv_halo = k.sb("cv_halo", [128, 2, 4, 30], F32)
        lc_halo = k.sb("lc_halo", [128, 2, 4, 3], F32)
        lru_st = k.sb("lru_st", [128, 2, 4], F32)
        kprev = k.sb("kprev", [128, 2, 2, 128], BF16)
        kmeta = k.sb("kmeta", [128, 2, 2, 16], BF16)
        vprev = k.sb("vprev", [128, 2, 4, 128], BF16)
        vmeta = k.sb("vmeta", [16, 2, 4, 128], BF16)
        NF = 22
        sga = k.sb("sga", [128, 8, TT], F32)
        F = [k.sb(f"F{i}", [128, TT], F32) if not (4 <= i < 12) else None for i in range(NF)]
        cg = k.sb("cg", [128, 4, 30 + TT], F32)
        lx = k.sb("lx", [128, 4, 3 + TT], F32)
        qT = k.sb("qT", [128, 8, TT], BF16)
        kT = k.sb("kT", [128, 2, 128 + TT], BF16)
        vaug = k.sb("vaug", [128, 4, 4, 128], BF16)
        NB = 6
        B = [k.sb(f"B{i}", [128, TT], BF16) for i in range(NB)]
        stat = k.sb("stat", [128, 4, 24], F32)
        mv = k.sb("mv", [128, 4, 2], F32)
        sm = k.sb("sm", [128, 16], F32)
        banks = [k.ps(f"P{i}", [128, 512], F32) for i in range(8)]
        bank_i = [0]

        aux_i = [0]

        def bank(aux=False):
            if aux:
                i = 4 + aux_i[0] % 2
                aux_i[0] += 1
            else:
                i = bank_i[0] % 4
                bank_i[0] += 1
            return banks[i], f"P{i}"

        ld = DmaSem(k, "ld")

        ldx = [DmaSem(k, f"ldx{i}") for i in range(4)]
        ldln = DmaSem(k, "ldln")
        ldpos = DmaSem(k, "ldpos")
        ld.group()
        ld.go(ident[:], ident_d, w=["ident"])
        ld.go(pc[:], pc_d, w=["pc"])
        ld.go(rotc[:], rc_d, w=["rotc"])
        ld.go(F[1][0:1, 0:32], sk_d, w=["F1"])
        ld.end()
        cst = DmaSem(k, "cst")
        cst.group()
        cst.go(permb[:], perm_d, w=["permb"], q="pool")
        cst.go(identb[:], ident_d, w=["identb"], q="pool")
        cst.go(maskb[:].rearrange("p a b -> p (a b)"), maskb_d, w=["maskb"], q="pool")
        cst.go(maskm[:], maskm_d, w=["maskm"], q="pool")
        cst.go(srow[:], srow_d, w=["srow"], q="pool")
        cst.end()
        smallw = DmaSem(k, "smallw")
        smallw.group()
        for l in range(2):
            smallw.go(pwb_d[l], pw_d[l], w=[("pwb_d", l)], q="pool")
            smallw.go(bdb_d[l], bd_d[l], w=[("bdb_d", l)], q="pool")
        smallw.end()
        ldsw = DmaSem(k, "ldsw")
        NPS = 8
        prep_sems = [DmaSem(k, f"prep{i}") for i in range(NPS)]
        prep_tok = {}
        pi = 0
        for l in range(2):
            for g in GROUP_ORDER:
                prep_tok[("in", l, g)] = prep_sems[pi % NPS].go(winb_d[l, g], win_d[l, g], w=[("winb", l, g), ("prepslot", pi % NPS)], q="pool")
                pi += 1
            for g in range(8):
                prep_tok[("out", l, g)] = prep_sems[pi % NPS].go(woutb_d[l, g], wout_d[l, g], w=[("woutb", l, g), ("prepslot", pi % NPS)], q="pool")
                pi += 1
        k.op("dve", lambda e: e.memset(onesf[:], 1.0 / 512.0), w=["onesf"])
        k.op("dve", lambda e: e.memset(cv_halo[:].rearrange("p a b c -> p (a b c)"), 0.0), w=["cv_halo0", "cv_halo1"])
        k.op("dve", lambda e: e.memset(lc_halo[:].rearrange("p a b c -> p (a b c)"), 0.0), w=["lc_halo0", "lc_halo1"])
        k.op("dve", lambda e: e.memset(lru_st[:].rearrange("p a b -> p (a b)"), 0.0), w=["lru_st0", "lru_st1"])
        k.op("pool", lambda e: e.memset(vaug[:].rearrange("p a b c -> p (a b c)"), 1.0), w=["vaug"])
        k.op("pool", lambda e: e.memset(vprev[:].rearrange("p a b c -> p (a b c)"), 1.0), w=["vprev0", "vprev1"])
        k.op("pool", lambda e: e.memset(vmeta[:].rearrange("p a b c -> p (a b c)"), 1.0), w=["vmeta0", "vmeta1"])
        k.op("dve", lambda e: e.tensor_scalar(out=pch[:], in0=pc[:], scalar1=0.5, scalar2=None, op0=ALU.mult),
             r=["pc"], w=["pch"])
        lam_ap = pc[:].rearrange("p (x j) -> p x j", j=NPCL)[:, :, 42]
        e_t, e2, acc_t = sm[:, 0:8], sm[:, 8:16], F[0][:, 0:8]
        k.op("act", lambda e: e.activation(out=e_t, in_=lam_ap, func=AF.Exp, scale=-1.0), r=["pc"], w=["sm"])
        k.op("dve", lambda e: e.tensor_scalar(out=acc_t, in0=e_t, scalar1=-0.25, scalar2=1.0 / 3.0, op0=ALU.mult, op1=ALU.add),
             r=["sm"], w=["F0"])
        k.op("dve", lambda e: e.tensor_tensor(out=e2, in0=e_t, in1=acc_t, op=ALU.mult), r=["sm", "F0"], w=["sm2"])
        k.op("dve", lambda e: e.tensor_scalar(out=acc_t, in0=e2, scalar1=-1.0, scalar2=0.5, op0=ALU.mult, op1=ALU.add),
             r=["sm2"], w=["F0"])
        k.op("dve", lambda e: e.tensor_tensor(out=e2, in0=e_t, in1=acc_t, op=ALU.mult), r=["sm", "F0"], w=["sm2"])
        k.op("dve", lambda e: e.tensor_scalar(out=acc_t, in0=e2, scalar1=-1.0, scalar2=1.0, op0=ALU.mult, op1=ALU.add),
             r=["sm2"], w=["F0"])
        k.op("dve", lambda e: e.tensor_tensor(out=e2, in0=e_t, in1=acc_t, op=ALU.mult), r=["sm", "F0"], w=["sm2"])
        k.op("dve", lambda e: e.tensor_scalar(out=hc[:], in0=e2, scalar1=-4.0, scalar2=None, op0=ALU.mult), r=["sm2"], w=["hc"])
        k.op("dve", lambda e: e.tensor_scalar(out=nhc[:], in0=e2, scalar1=4.0, scalar2=None, op0=ALU.mult), r=["sm2"], w=["nhc"])
        k.op("act", lambda e: e.activation(out=sinkp[0:1, :], in_=F[1][0:1, 0:32], func=AF.Exp), r=["F1"], w=["sinkp"])
        items = []
        tiles = [("meta", NMETA, 0, 0)] + [("real", TT, NMETA + TT * i, i) for i in range(n_real)]
        for _ in tiles:
            for l in range(2):
                for g in GROUP_ORDER:
                    items.append(("in", l, g))
                for g in range(8):
                    items.append(("out", l, g))
        wsem = [DmaSem(k, f"w{i}") for i in range(NSLOT)]
        wstate = {"next_load": 0, "next_use": 0}

        def w_issue():
            n = wstate["next_load"]
            if n >= len(items):
                return
            kind, l, g = items[n]
            slot = n % NSLOT
            src = (winb_d if kind == "in" else woutb_d)[l, g]
            reg = ("winb", l, g) if kind == "in" else ("woutb", l, g)
            wsem[slot].go(wbuf[slot][:].rearrange("p a b -> p (a b)"), src, r=[reg], w=[f"wbuf{slot}"])
            wstate["next_load"] = n + 1

        def w_get(expect):
            n = wstate["next_use"]
            assert items[n] == expect, (items[n], expect)
            wstate["next_use"] = n + 1
            slot = n % NSLOT
            return wbuf[slot], f"wbuf{slot}"

        for _ in range(NSLOT):
            w_issue()

        def pcs(l, c, j, half=False):
            i = pc_index(l, c, j)
            return (pch if half else pc)[:, i:i + 1]

        lnp_loaded = [False]
        unit_i = [0]
        ln_calls = [0]

        def layer_norm_rows(T, nsub, P, gb_idx, stats_done=False, staged=False):
            if not lnp_loaded[0]:
                ldln.go(lnp[:].rearrange("p a b -> p (a b)"), lnp_d[gb_idx].rearrange("p a b -> p (a b)"), w=["lnp"])
                lnp_loaded[0] = True
            for s in range(nsub):
                if not stats_done:
                    for c4 in range(4):
                        src_ap = xs[s][0:P, c4 * 512:(c4 + 1) * 512] if staged else h[0:P, s, c4 * 512:(c4 + 1) * 512]
                        k.op("dve", lambda e, s=s, c4=c4, src_ap=src_ap: e.bn_stats(out=stat[0:P, s, c4 * 6:(c4 + 1) * 6], in_=src_ap),
                             r=(xs_reg[s] if staged else [("h", s)]), w=[("stat", s, c4)])
                    k.op("dve", lambda e, s=s: e.bn_aggr(out=mv[0:P, s, :], in_=stat[0:P, s, 0:24]),
                         r=[("stat", s, c4) for c4 in range(4)], w=[("mv", s)])
                else:
                    k.op("dve", lambda e, s=s: e.bn_aggr(out=mv[0:P, s, :], in_=stat[0:P, s, 0:24]),
                         r=[("stat", s, c4) for c4 in range(4)], w=[("mv", s)])
            k.op("dve", lambda e: e.tensor_scalar(out=sm[0:P, 0:nsub], in0=mv[0:P, 0:nsub, 1], scalar1=LN_EPS, scalar2=None, op0=ALU.add),
                 r=[("mv", s) for s in range(nsub)], w=["sm"])
            k.op("act", lambda e: e.activation(out=sm[0:P, 4:4 + nsub], in_=sm[0:P, 0:nsub], func=AF.Sqrt), r=["sm"], w=["sm_b"])
            k.op("dve", lambda e: e.reciprocal(out=sm[0:P, 8:8 + nsub], in_=sm[0:P, 4:4 + nsub]), r=["sm_b"], w=["sm_c"])
            k.op("dve", lambda e: e.scalar_tensor_tensor(out=sm[0:P, 12:12 + nsub], in0=mv[0:P, 0:nsub, 0], scalar=-1.0,
                                                         in1=sm[0:P, 8:8 + nsub], op0=ALU.mult, op1=ALU.mult),
                 r=["sm_c"] + [("mv", s) for s in range(nsub)], w=["sm_d"])
            for s in range(nsub):
                src_ap = xs[s][0:P, :] if staged else h[0:P, s, :]
                k.op("act", lambda e, s=s, src_ap=src_ap: e.activation(out=h[0:P, s, :], in_=src_ap, func=AF.Identity,
                                                        scale=sm[0:P, 8 + s:9 + s], bias=sm[0:P, 12 + s:13 + s]),
                     r=(xs_reg[s] if staged else [("h", s)]) + ["sm_c", "sm_d"], w=[("h", s)])
                k.op("dve", lambda e, s=s: e.tensor_tensor(out=h[0:P, s, :], in0=h[0:P, s, :], in1=lnp[0:P, 0, :], op=ALU.mult),
                     r=[("h", s), "lnp"], w=[("h", s)])
                k.op("dve", lambda e, s=s: e.tensor_tensor(out=h[0:P, s, :], in0=h[0:P, s, :], in1=lnp[0:P, 1, :], op=ALU.add),
                     r=[("h", s), "lnp"], w=[("h", s)])
            nxt = (gb_idx + 1) % 3
            ln_calls[0] += 1
            if ln_calls[0] < 3 * len(tiles):
                ldln.go(lnp[:].rearrange("p a b -> p (a b)"), lnp_d[nxt].rearrange("p a b -> p (a b)"), w=["lnp"])

        def proj_fm(wt, wreg, sub, T):
            pb, preg = bank()
            for kc in range(NKC):
                k.op("pe", lambda e, kc=kc: e.matmul(pb[:, 0:T], lhsT=wt[:, kc, sub * 128:(sub + 1) * 128], rhs=hT[:, kc, 0:T],
                                                     start=(kc == 0), stop=(kc == NKC - 1)),
                     r=[wreg, ("hT", kc)], w=[preg])
            return pb, preg

        def gate2silu(pb, preg, out_ap, out_reg, tmp, tmp_reg, T):
            k.op("act", lambda e: e.activation(out=out_ap, in_=pb[:, 0:T], func=AF.Silu), r=[preg], w=[out_reg])

        st_out = [DmaSem(k, f"st_out{i}") for i in range(4)]
        dbgs = DmaSem(k, "dbgs") if dbg else None

        for (tkind, T, pos0, ridx) in tiles:
            is_meta = tkind == "meta"
            nsub = 1 if is_meta else 4
            P = NMETA if is_meta else 128
            nblk = 0 if is_meta else 4
            if is_meta:
                ldx[0].go(h[0:NMETA, 0, :], meta_d, w=[("h", 0)])
            layer_norm_rows(T, nsub, P, 0, staged=not is_meta)
            nxt_ridx = 0 if is_meta else ridx + 1
            ldpos.go(F[0][:, 0:T], pos_d[:, pos0:pos0 + T], w=["F0"])
            k.op("dve", lambda e: e.tensor_scalar(out=F[1][:, 0:T], in0=F[0][:, 0:T], scalar1=rotc[:, 0:1], scalar2=None, op0=ALU.mult),
                 r=["F0", "rotc"], w=["F1"])
            for which, dst, dreg in ((0, St, "St"), (1, Ct, "Ct")):
                shift = 0.0 if which == 0 else math.pi / 2
                k.op("dve", lambda e, shift=shift: e.tensor_scalar(out=F[2][:, 0:T], in0=F[1][:, 0:T], scalar1=shift, scalar2=None, op0=ALU.add),
                     r=["F1"], w=["F2"])
                k.op("dve", lambda e: e.tensor_scalar(out=F[3][:, 0:T], in0=F[2][:, 0:T], scalar1=1.0 / (2 * math.pi), scalar2=None, op0=ALU.mult),
                     r=["F2"], w=["F3"])
                k.op("dve", lambda e: e.tensor_copy(out=F[0][:, 0:T].bitcast(I32), in_=F[3][:, 0:T]), r=["F3"], w=["F0"])
                k.op("dve", lambda e: e.tensor_copy(out=F[3][:, 0:T], in_=F[0][:, 0:T].bitcast(I32)), r=["F0"], w=["F3"])
                k.op("dve", lambda e: e.scalar_tensor_tensor(out=F[2][:, 0:T], in0=F[3][:, 0:T], scalar=-6.28125, in1=F[2][:, 0:T],
                                                             op0=ALU.mult, op1=ALU.add), r=["F3", "F2"], w=["F2"])
                k.op("dve", lambda e: e.scalar_tensor_tensor(out=F[2][:, 0:T], in0=F[3][:, 0:T], scalar=-(2 * math.pi - 6.28125), in1=F[2][:, 0:T],
                                                             op0=ALU.mult, op1=ALU.add), r=["F3", "F2"], w=["F2"])
                k.op("dve", lambda e: e.tensor_scalar(out=F[2][:, 0:T], in0=F[2][:, 0:T], scalar1=-3.1415925, scalar2=3.1415925, op0=ALU.max, op1=ALU.min),
                     r=["F2"], w=["F2"])
                k.op("act", lambda e, dst=dst: e.activation(out=dst[:, 0:T], in_=F[2][:, 0:T], func=AF.Sin), r=["F2"], w=[dreg])
            k.op("dve", lambda e: e.tensor_scalar(out=St[:, 0:T], in0=St[:, 0:T], scalar1=rotc[:, 1:2], scalar2=None, op0=ALU.mult),
                 r=["St", "rotc"], w=["St"])
            if dbg and ridx == 0 and not is_meta:
                dbgs.go(dbg_d["d_h0"], h[:].rearrange("p a b -> p (a b)"), r=[("h", s) for s in range(4)])

            for l in range(2):
                ldsw.group()
                ldsw.go(pwb[:].rearrange("p a b -> p (a b)"), pwb_d[l], r=[("pwb_d", l)], w=["pwb"])
                ldsw.go(bdb[:].rearrange("p g c m -> p (g c m)"), bdb_d[l], r=[("bdb_d", l)], w=["bdb"])
                ldsw.end()
                k.cur_tag = "hT"
                ev = 0
                for s in range(nsub):
                    j = s % 2
                    hb = qT[:, 4 * j:4 * j + 4, :].rearrange("p a b -> p (a b)")
                    hbreg = [("qT", 4 * j + q) for q in range(4)]
                    if s % 2 == 0:
                        k.op("act", lambda e, s=s, hb=hb: e.activation(out=hb[0:P, :], in_=h[0:P, s, :], func=AF.Copy), r=[("h", s)], w=hbreg)
                    else:
                        k.op("dve", lambda e, s=s, hb=hb: e.tensor_copy(out=hb[0:P, :], in_=h[0:P, s, :]), r=[("h", s)], w=hbreg)
                    for k4 in range(4):
                        pb, preg = bank()
                        pbv = pb[:, 0:256].bitcast(BF16)
                        for kk in range(4):
                            kc = k4 * 4 + kk
                            k.op("pe", lambda e, kc=kc, kk=kk, pbv=pbv, hb=hb: e.transpose(pbv[:, kk * 128: kk * 128 + P], hb[0:P, kc * 128:(kc + 1) * 128], identb[0:P, 0:P]),
                                 r=hbreg + ["identb"], w=[preg])
                        src = pbv.rearrange("p (a b) -> p a b", a=4)[:, :, 0:P]
                        dst = hT[:, k4 * 4:(k4 + 1) * 4, s * 128:s * 128 + P]
                        regs = [("hT", k4 * 4 + kk) for kk in range(4)]
                        if ev % 2 == 0:
                            k.op("act", lambda e, src=src, dst=dst: e.activation(out=dst, in_=src, func=AF.Copy), r=[preg], w=regs)
                        else:
                            k.op("dve", lambda e, src=src, dst=dst: e.tensor_copy(out=dst, in_=src), r=[preg], w=regs)
                        ev += 1

                k.cur_tag = "conv1"
                for c in range(4):
                    k.op("act", lambda e, c=c: e.activation(out=cg[:, c, 0:30], in_=cv_halo[:, l, c, :], func=AF.Copy), r=[f"cv_halo{l}"], w=[("cg", c)])
                for gi in range(2):
                    wt, wreg = w_get(("in", l, 2 + gi))
                    for sub in range(2):
                        pb, preg = proj_fm(wt, wreg, sub, T)
                        tmp = F[sub]
                        k.op("act", lambda e, pb=pb, tmp=tmp: e.activation(out=tmp[:, 0:T], in_=pb[:, 0:T], func=AF.Tanh, scale=0.5),
                             r=[preg], w=[f"F{sub}"])
                    w_issue()
                    wt, wreg = w_get(("in", l, 0 + gi))
                    for sub in range(2):
                        c = gi * 2 + sub
                        pb, preg = proj_fm(wt, wreg, sub, T)
                        k.op("dve", lambda e, pb=pb, c=c, sub=sub: e.scalar_tensor_tensor(out=cg[:, c, 30:30 + T], in0=F[sub][:, 0:T], scalar=1.0,
                                                                                      in1=pb[:, 0:T], op0=ALU.add, op1=ALU.mult),
                             r=[f"F{sub}", preg], w=[("cg", c)])
                    w_issue()
                pm, pmreg = banks[6], "P6"
                pq, pqreg = banks[7], "P7"
                for c in range(4):
                    A, Areg = F[12 + c], f"F{12 + c}"
                    Bq, Breg = F[16 + c % 2], f"F{16 + c % 2}"
                    k.op("dve", lambda e, c=c, A=A: e.tensor_scalar(out=A[:, 0:T], in0=cg[:, c, 0:T], scalar1=pcs(l, c, 0, True), scalar2=pcs(l, c, 31),
                                                                    op0=ALU.mult, op1=ALU.add), r=[("cg", c), "pc", "pch"], w=[Areg])
                    for tap in range(1, 31):
                        k.op("dve", lambda e, c=c, A=A, tap=tap: e.scalar_tensor_tensor(out=A[:, 0:T], in0=cg[:, c, tap:tap + T], scalar=pcs(l, c, tap, True),
                                                                                    in1=A[:, 0:T], op0=ALU.mult, op1=ALU.add),
                             r=[("cg", c), "pch", Areg], w=[Areg])
                    k.op("act", lambda e, c=c: e.activation(out=cv_halo[:, l, c, :], in_=cg[:, c, T:T + 30], func=AF.Copy), r=[("cg", c)], w=[f"cv_halo{l}"])
                    k.op("act", lambda e, A=A, Bq=Bq: e.activation(out=Bq[:, 0:T], in_=A[:, 0:T], func=AF.Square), r=[Areg], w=[Breg])
                    k.op("pe", lambda e, c=c, A=A: e.matmul(pm[:, 0:T], lhsT=onesf[:, :], rhs=A[:, 0:T], start=(c == 0), stop=(c == 3)),
                         r=["onesf", Areg], w=[pmreg])
                    k.op("pe", lambda e, c=c, Bq=Bq: e.matmul(pq[:, 0:T], lhsT=onesf[:, :], rhs=Bq[:, 0:T], start=(c == 0), stop=(c == 3)),
                         r=["onesf", Breg], w=[pqreg])
                k.op("act", lambda e: e.activation(out=F[0][:, 0:T], in_=pm[:, 0:T], func=AF.Copy), r=[pmreg], w=["F0"])
                k.op("dve", lambda e: e.tensor_tensor(out=F[1][:, 0:T], in0=F[0][:, 0:T], in1=F[0][:, 0:T], op=ALU.mult), r=["F0"], w=["F1"])
                k.op("dve", lambda e: e.scalar_tensor_tensor(out=F[1][:, 0:T], in0=F[1][:, 0:T], scalar=-1.0, in1=pq[:, 0:T], op0=ALU.mult, op1=ALU.add),
                     r=["F1", pqreg], w=["F1"])
                k.op("dve", lambda e: e.tensor_scalar(out=F[1][:, 0:T], in0=F[1][:, 0:T], scalar1=LN_EPS, scalar2=None, op0=ALU.add), r=["F1"], w=["F1"])
                k.op("act", lambda e: e.activation(out=F[1][:, 0:T], in_=F[1][:, 0:T], func=AF.Sqrt), r=["F1"], w=["F1"])
                k.op("dve", lambda e: e.reciprocal(out=F[1][:, 0:T], in_=F[1][:, 0:T]), r=["F1"], w=["F1"])
                k.op("dve", lambda e: e.scalar_tensor_tensor(out=F[0][:, 0:T], in0=F[0][:, 0:T], scalar=-1.0, in1=F[1][:, 0:T], op0=ALU.mult, op1=ALU.mult),
                     r=["F0", "F1"], w=["F0"])
                for c in range(4):
                    A, Areg = F[12 + c], f"F{12 + c}"
                    Bq, Breg = F[16 + c % 2], f"F{16 + c % 2}"
                    k.op("dve", lambda e, A=A: e.tensor_tensor(out=A[:, 0:T], in0=A[:, 0:T], in1=F[1][:, 0:T], op=ALU.mult), r=[Areg, "F1"], w=[Areg])
                    k.op("dve", lambda e, A=A: e.tensor_tensor(out=A[:, 0:T], in0=A[:, 0:T], in1=F[0][:, 0:T], op=ALU.add), r=[Areg, "F0"], w=[Areg])
                    k.op("act", lambda e, A=A, Bq=Bq, c=c: e.activation(out=Bq[:, 0:T], in_=A[:, 0:T], func=AF.Tanh, scale=pcs(l, c, 32, True), bias=pcs(l, c, 33, True)),
                         r=[Areg, "pch"], w=[Breg])
                    k.op("dve", lambda e, A=A, c=c: e.tensor_scalar(out=A[:, 0:T], in0=A[:, 0:T], scalar1=pcs(l, c, 32, True), scalar2=pcs(l, c, 33, True), op0=ALU.mult, op1=ALU.add),
                         r=[Areg, "pch"], w=[Areg])
                    k.op("dve", lambda e, A=A, Bq=Bq, c=c: e.scalar_tensor_tensor(out=B[c][:, 0:T], in0=Bq[:, 0:T], scalar=1.0, in1=A[:, 0:T], op0=ALU.add, op1=ALU.mult),
                         r=[Areg, Breg], w=[f"B{c}"])

                k.cur_tag = "lru"
                for c in range(4):
                    k.op("act", lambda e, c=c: e.activation(out=lx[:, c, 0:3], in_=lc_halo[:, l, c, :], func=AF.Copy), r=[f"lc_halo{l}"], w=[("lx", c)])
                for gi in range(2):
                    wt, wreg = w_get(("in", l, 16 + gi))
                    for sub in range(2):
                        c = gi * 2 + sub
                        pb, preg = proj_fm(wt, wreg, sub, T)
                        k.op("act", lambda e, pb=pb, c=c: e.activation(out=lx[:, c, 3:3 + T], in_=pb[:, 0:T], func=AF.Copy), r=[preg], w=[("lx", c)])
                    w_issue()
                for gi in range(2):
                    wt, wreg = w_get(("in", l, 18 + gi))
                    for sub in range(2):
                        c = gi * 2 + sub
                        pb, preg = proj_fm(wt, wreg, sub, T)
                        gate2silu(pb, preg, F[18 + c][:, 0:T], f"F{18 + c}", F[2 + sub], f"F{2 + sub}", T)
                    w_issue()
                for c in range(4):
                    X, Xreg = F[12], "F12"
                    k.op("dve", lambda e, c=c: e.tensor_scalar(out=X[:, 0:T], in0=lx[:, c, 0:T], scalar1=pcs(l, c, 35), scalar2=pcs(l, c, 39), op0=ALU.mult, op1=ALU.add),
                         r=[("lx", c), "pc"], w=[Xreg])
                    for tap in range(1, 4):
                        k.op("dve", lambda e, c=c, tap=tap: e.scalar_tensor_tensor(out=X[:, 0:T], in0=lx[:, c, tap:tap + T], scalar=pcs(l, c, 35 + tap), in1=X[:, 0:T],
                                                                               op0=ALU.mult, op1=ALU.add), r=[("lx", c), "pc", Xreg], w=[Xreg])
                    k.op("act", lambda e, c=c: e.activation(out=lc_halo[:, l, c, :], in_=lx[:, c, T:T + 3], func=AF.Copy), r=[("lx", c)], w=[f"lc_halo{l}"])
                    k.op("act", lambda e: e.activation(out=B[4][:, 0:T], in_=X[:, 0:T], func=AF.Copy), r=[Xreg], w=["B4"])
                    pa, pareg = bank(aux=True)
                    px, pxreg = bank(aux=True)
                    k.op("pe", lambda e, c=c, pa=pa: e.matmul(pa[:, 0:T], lhsT=bdb[:, 0, c, :], rhs=B[4][:, 0:T], start=True, stop=True), r=["bdb", "B4"], w=[pareg])
                    k.op("pe", lambda e, c=c, px=px: e.matmul(px[:, 0:T], lhsT=bdb[:, 1, c, :], rhs=B[4][:, 0:T], start=True, stop=True), r=["bdb", "B4"], w=[pxreg])
                    k.op("act", lambda e, c=c, pa=pa: e.activation(out=F[13][:, 0:T], in_=pa[:, 0:T], func=AF.Tanh, scale=0.5, bias=pcs(l, c, 40, True)), r=[pareg, "pch"], w=["F13"])
                    k.op("act", lambda e, c=c, px=px: e.activation(out=F[14][:, 0:T], in_=px[:, 0:T], func=AF.Tanh, scale=0.5, bias=pcs(l, c, 41, True)), r=[pxreg, "pch"], w=["F14"])
                    hci = l * 4 + c
                    k.op("act", lambda e, hci=hci: e.activation(out=F[15][:, 0:T], in_=F[13][:, 0:T], func=AF.Exp, scale=hc[:, hci:hci + 1], bias=hc[:, hci:hci + 1]), r=["F13", "hc"], w=["F15"])
                    k.op("act", lambda e, hci=hci: e.activation(out=F[16][:, 0:T], in_=F[13][:, 0:T], func=AF.Tanh, scale=nhc[:, hci:hci + 1], bias=nhc[:, hci:hci + 1]), r=["F13", "nhc"], w=["F16"])
                    k.op("act", lambda e: e.activation(out=F[17][:, 0:T], in_=F[15][:, 0:T], func=AF.Square), r=["F15"], w=["F17"])
                    k.op("dve", lambda e: e.scalar_tensor_tensor(out=F[16][:, 0:T], in0=F[17][:, 0:T], scalar=1.0, in1=F[16][:, 0:T], op0=ALU.add, op1=ALU.mult), r=["F17", "F16"], w=["F16"])
                    k.op("act", lambda e: e.activation(out=F[16][:, 0:T], in_=F[16][:, 0:T], func=AF.Sqrt, scale=0.25), r=["F16"], w=["F16"])
                    k.op("dve", lambda e: e.scalar_tensor_tensor(out=F[14][:, 0:T], in0=F[14][:, 0:T], scalar=1.0, in1=X[:, 0:T], op0=ALU.add, op1=ALU.mult), r=["F14", Xreg], w=["F14"])
                    k.op("dve", lambda e: e.tensor_tensor(out=F[14][:, 0:T], in0=F[14][:, 0:T], in1=F[16][:, 0:T], op=ALU.mult), r=["F14", "F16"], w=["F14"])
                    k.op("dve", lambda e, c=c: e.tensor_tensor_scan(out=F[17][:, 0:T], data0=F[15][:, 0:T], data1=F[14][:, 0:T], initial=lru_st[:, l, c:c + 1],
                                                                   op0=ALU.mult, op1=ALU.add), r=["F15", "F14", f"lru_st{l}"], w=["F17"])
                    k.op("dve", lambda e, c=c: e.tensor_copy(out=lru_st[:, l, c:c + 1], in_=F[17][:, T - 1:T]), r=["F17"], w=[f"lru_st{l}"])
                    k.op("dve", lambda e, c=c: e.scalar_tensor_tensor(out=yT[:, 12 + c, 0:T], in0=F[17][:, 0:T], scalar=1.0, in1=F[18 + c][:, 0:T], op0=ALU.mult, op1=ALU.mult),
                         r=["F17", f"F{18 + c}"], w=[("yT", 12 + c)])

                k.cur_tag = "qkv"
                if not is_meta and ridx > 0:
                    k.op("act", lambda e: e.activation(out=kT[:, :, 0:128], in_=kprev[:, l, :, :], func=AF.Copy), r=[f"kprev{l}"], w=["kT"])

                rot_list = []

                def rot_tile(pb, preg, out_ap, out_reg):
                    k.op("act", lambda e: e.activation(out=out_ap, in_=pb[:, 0:T], func=AF.Copy), r=[preg], w=[out_reg])
                    rot_list.append((out_ap, out_reg))

                def rot_pass():
                    for i, (ap, reg) in enumerate(rot_list[8:] + rot_list[:8]):
                        fa, fareg = (F[2], "F2")
                        fb, fbreg = (F[3], "F3")
                        p2, p2reg = bank(aux=True)
                        k.op("pe", lambda e, ap=ap, p2=p2: e.matmul(p2[:, 0:T], lhsT=permb[:, :], rhs=ap, start=True, stop=True), r=["permb", reg], w=[p2reg])
                        k.op("dve", lambda e, p2=p2: e.tensor_tensor(out=fb[:, 0:T], in0=p2[:, 0:T], in1=St[:, 0:T], op=ALU.mult), r=[p2reg, "St"], w=[fbreg], pri=0)
                        k.op("dve", lambda e, ap=ap: e.tensor_tensor(out=fa[:, 0:T], in0=ap, in1=Ct[:, 0:T], op=ALU.mult), r=[reg, "Ct"], w=[fareg], pri=0)
                        k.op("dve", lambda e, ap=ap: e.tensor_tensor(out=ap, in0=fa[:, 0:T], in1=fb[:, 0:T], op=ALU.add), r=[fareg, fbreg], w=[reg], pri=0)

                for gi in range(4):
                    wt, wreg = w_get(("in", l, 6 + gi))
                    for sub in range(2):
                        i = gi * 2 + sub
                        pb, preg = proj_fm(wt, wreg, sub, T)
                        rot_tile(pb, preg, qT[:, i, 0:T], ("qT", i))
                    w_issue()
                wt, wreg = w_get(("in", l, 10))
                for sub in range(2):
                    pb, preg = proj_fm(wt, wreg, sub, T)
                    if is_meta:
                        rot_tile(pb, preg, kmeta[:, l, sub, 0:T], f"kmeta{l}")
                    else:
                        rot_tile(pb, preg, kT[:, sub, 128:128 + T], "kT")
                w_issue()
                wt, wreg = w_get(("in", l, 11))
                for b in range(max(nblk, 1)):
                    pb, preg = bank()
                    for kc in range(NKC):
                        k.op("pe", lambda e, kc=kc, b=b, pb=pb: e.matmul(pb[0:P, 0:256], lhsT=hT[:, kc, b * 128:b * 128 + P], rhs=wt[:, kc, 0:256],
                                                                     start=(kc == 0), stop=(kc == NKC - 1)), r=[wreg, ("hT", kc)], w=[preg])
                    for j in range(4):
                        off = 0 if j % 2 == 0 else 64
                        if is_meta:
                            k.op("act", lambda e, j=j, off=off, pb=pb: e.activation(out=vmeta[0:P, l, j, off:off + 64], in_=pb[0:P, j * 64:(j + 1) * 64], func=AF.Copy),
                                 r=[preg], w=[f"vmeta{l}"])
                        else:
                            k.op("act", lambda e, j=j, off=off, pb=pb, b=b: e.activation(out=vaug[:, b, j, off:off + 64], in_=pb[:, j * 64:(j + 1) * 64], func=AF.Copy),
                                 r=[preg], w=["vaug"])
                w_issue()
                k.cur_tag = "rot"
                rot_pass()
                k.cur_tag = "gates"
                for gi in range(4):
                    wt, wreg = w_get(("in", l, 12 + gi))
                    for sub in range(2):
                        i = gi * 2 + sub
                        pb, preg = proj_fm(wt, wreg, sub, T)
                        gate2silu(pb, preg, sga[:, i, 0:T], ("sga", i), None, None, T)
                    w_issue()
                k.cur_tag = "conv2"
                for gi in range(2):
                    wt, wreg = w_get(("in", l, 4 + gi))
                    for sub in range(2):
                        co = gi * 2 + sub
                        pg, pgreg = proj_fm(wt, wreg, sub, T)
                        G, Greg = F[sub], f"F{sub}"
                        gate2silu(pg, pgreg, G[:, 0:T], Greg, F[2 + sub], f"F{2 + sub}", T)
                        pb, preg = bank(aux=True)
                        for ci in range(4):
                            k.op("pe", lambda e, ci=ci, co=co, pb=pb: e.matmul(pb[:, 0:T], lhsT=pwb[:, ci, co * 128:(co + 1) * 128], rhs=B[ci][:, 0:T],
                                                                          start=(ci == 0), stop=(ci == 3)), r=["pwb", f"B{ci}"], w=[preg])
                        k.op("dve", lambda e, co=co, pb=pb, G=G: e.scalar_tensor_tensor(out=yT[:, co, 0:T], in0=pb[:, 0:T], scalar=pcs(l, co, 34), in1=G[:, 0:T],
                                                                              op0=ALU.add, op1=ALU.mult), r=[preg, "pc", Greg], w=[("yT", co)])
                    w_issue()

                if l == 1 and nxt_ridx < n_real:
                    for s in (0, 1):
                        ldx[s].go(xs[s], x_d[nxt_ridx * TT + s * 128: nxt_ridx * TT + (s + 1) * 128, :], w=xs_reg[s])
                k.cur_tag = "units"
                nq = NMETA if is_meta else 128
                for b in range(max(nblk, 1)):
                    for j in range(4):
                        half = j % 2
                        hs = slice(half * 64, half * 64 + 64)
                        ds = slice((1 - half) * 64, (1 - half) * 64 + 64)
                        t0 = 4 * (j // 2)
                        q_rhs = qT[hs, t0:t0 + 4, b * 128:b * 128 + nq]
                        NQ = 4 * nq
                        kblocks = []
                        if is_meta:
                            kblocks.append((kmeta[hs, l, j // 2, 0:NMETA], NMETA, maskm[0:NMETA, 0:64], vmeta[0:NMETA, l, j, :], [f"kmeta{l}", f"vmeta{l}", "maskm"]))
                        else:
                            kblocks.append((kmeta[hs, l, j // 2, 0:NMETA], NMETA, None, vmeta[0:NMETA, l, j, :], [f"kmeta{l}", f"vmeta{l}"]))
                            if not (ridx == 0 and b == 0):
                                kblocks.append((kT[hs, j // 2, b * 128:(b + 1) * 128], 128, maskb[:, 0:1, :].to_broadcast([128, 4, 128]), (vprev[:, l, j, :] if b == 0 else vaug[:, b - 1, j, :]), ["kT", "vaug", f"vprev{l}", "maskb"]))
                            kblocks.append((kT[hs, j // 2, (b + 1) * 128:(b + 2) * 128], 128, maskb[:, 1:2, :].to_broadcast([128, 4, 128]), vaug[:, b, j, :], ["kT", "vaug", "maskb"]))
                        ex = []
                        for bi, (k_l, nk, m_ap, v_l, regs) in enumerate(kblocks):
                            pb, preg = bank()
                            k.op("pe", lambda e, pb=pb, k_l=k_l, nk=nk, m_ap=m_ap: e.matmul(pb[0:nk, 0:NQ], lhsT=k_l, rhs=q_rhs, start=True, stop=(m_ap is None)),
                                 r=regs + [("qT", t0 + g) for g in range(4)], w=[preg])
                            if m_ap is not None:
                                k.op("pe", lambda e, pb=pb, nk=nk, m_ap=m_ap: e.matmul(pb[0:nk, 0:NQ], lhsT=identb[0:nk, 0:nk], rhs=m_ap, start=False, stop=True),
                                     r=regs + ["identb"], w=[preg])
                            ebi = (unit_i[0] % 2) * 3 + bi
                            eb = B[ebi]
                            k.op("act", lambda e, pb=pb, nk=nk, eb=eb: e.activation(out=eb[0:nk, 0:NQ], in_=pb[0:nk, 0:NQ], func=AF.Exp, scale=0.125), r=[preg], w=[f"B{ebi}"])
                            ex.append((eb, f"B{ebi}", nk, v_l, regs))
                        unit_i[0] += 1
                        po, poreg = bank(aux=True)
                        for bi, (eb, ereg, nk, v_l, regs) in enumerate(ex):
                            k.op("pe", lambda e, eb=eb, nk=nk, v_l=v_l, bi=bi: e.matmul(po[:, 0:NQ], lhsT=v_l, rhs=eb[0:nk, 0:NQ], start=(bi == 0), stop=False),
                                 r=regs + [ereg], w=[poreg])
                        sp_ap = sinkp[0:1, (l * 4 + j) * 4:(l * 4 + j) * 4 + 4].rearrange("p (g o) -> p g o", o=1).to_broadcast([1, 4, nq])
                        k.op("pe", lambda e: e.matmul(po[:, 0:NQ], lhsT=srow[0:1, half * 128:(half + 1) * 128], rhs=sp_ap, start=False, stop=True),
                             r=["srow", "sinkp"], w=[poreg])
                        k.op("act", lambda e: e.activation(out=F[2][ds, 0:NQ], in_=po[ds, 0:NQ], func=AF.Ln), r=[poreg], w=["F2"])
                        k.op("act", lambda e: e.activation(out=F[0][ds, 0:NQ], in_=F[2][ds, 0:NQ], func=AF.Exp, scale=-1.0), r=["F2"], w=["F0"])
                        k.op("dve", lambda e: e.tensor_tensor(out=F[1][hs, 0:NQ], in0=po[hs, 0:NQ], in1=F[0][ds, 0:NQ], op=ALU.mult), r=[poreg, "F0"], w=["F1"])
                        k.op("dve", lambda e: e.tensor_tensor(out=yT[hs, 4 + t0:4 + t0 + 4, b * 128:b * 128 + nq],
                                                               in0=F[1][hs, 0:NQ].rearrange("p (g q) -> p g q", g=4),
                                                               in1=sga[hs, t0:t0 + 4, b * 128:b * 128 + nq], op=ALU.mult),
                             r=["F1"] + [("sga", t0 + g) for g in range(4)], w=[("yT", 4 + t0 + g) for g in range(4)])
                if not is_meta:
                    k.op("act", lambda e: e.activation(out=kprev[:, l, :, :], in_=kT[:, :, TT:TT + 128], func=AF.Copy), r=["kT"], w=[f"kprev{l}"])
                    k.op("act", lambda e: e.activation(out=vprev[:, l, :, :], in_=vaug[:, 3, :, :], func=AF.Copy), r=["vaug"], w=[f"vprev{l}"])
                if dbg and ridx == 0 and not is_meta and l == 0:
                    dbgs.go(dbg_d["d_yT"], yT[:].rearrange("p a b -> p (a b)"), r=[("yT", i) for i in range(16)])

                k.cur_tag = "wout"
                for g in range(8):
                    wt, wreg = w_get(("out", l, g))
                    for s in range(nsub):
                        pb, preg = bank()
                        for kc in range(NKC):
                            k.op("pe", lambda e, kc=kc, s=s, pb=pb: e.matmul(pb[0:P, 0:256], lhsT=yT[:, kc, s * 128:s * 128 + P], rhs=wt[:, kc, 0:256],
                                                                         start=(kc == 0), stop=(kc == NKC - 1)), r=[wreg, ("yT", kc)], w=[preg])
                        k.op("dve", lambda e, s=s, g=g, pb=pb: e.scalar_tensor_tensor(out=h[0:P, s, g * 256:(g + 1) * 256], in0=h[0:P, s, g * 256:(g + 1) * 256], scalar=ALPHA,
                                                                                 in1=pb[0:P, 0:256], op0=ALU.mult, op1=ALU.add), r=[("h", s), preg], w=[("h", s)])
                        if g % 2 == 1:
                            c4 = g // 2
                            k.op("dve", lambda e, s=s, c4=c4: e.bn_stats(out=stat[0:P, s, c4 * 6:(c4 + 1) * 6], in_=h[0:P, s, c4 * 512:(c4 + 1) * 512]),
                                 r=[("h", s)], w=[("stat", s, c4)])
                    w_issue()
                if l == 1 and nxt_ridx < n_real:
                    for s in (2, 3):
                        ldx[s].go(xs[s], x_d[nxt_ridx * TT + s * 128: nxt_ridx * TT + (s + 1) * 128, :], w=xs_reg[s])
                layer_norm_rows(T, nsub, P, 1 + l, stats_done=True)
                if dbg and ridx == 0 and not is_meta and l == 0:
                    dbgs.go(dbg_d["d_h1"], h[:].rearrange("p a b -> p (a b)"), r=[("h", s) for s in range(4)])
            if not is_meta:
                for s in range(4):
                    st_out[s].go(y_d[ridx * TT + s * 128: ridx * TT + (s + 1) * 128, :], h[:, s, :], r=[("h", s)])
        for s in range(4):
            k.wait_final("sp", st_out[s].sem, 16 * st_out[s].n)
        if dbg:
            k.wait_final("sp", dbgs.sem, 16 * dbgs.n)
        k.finish()
    return nc


def _host_consts(L):
    ident = np.eye(128, dtype=np.float32)
    perm = np.zeros((128, 128), np.float32)
    for m in range(128):
        d = m % 64
        if d < 8:
            perm[m + 8, m] = 1.0
        elif d < 16:
            perm[m - 8, m] = 1.0
    kj = np.arange(128)[:, None]
    qi = np.arange(128)[None, :]
    mprev = np.where(kj > qi, 0.0, MASKV).astype(np.float32)
    mcur = np.where(kj <= qi, 0.0, MASKV).astype(np.float32)
    maskb = np.stack([mprev, mcur], axis=1).reshape(128, 256)
    km = np.arange(16)[:, None]
    qm = np.arange(16)[None, :]
    maskm = np.tile(np.where(km <= qm, 0.0, MASKV).astype(np.float32), (1, 4))
    pos = np.tile(np.arange(L, dtype=np.float32)[None, :], (128, 1))
    half = 8
    inv_freq = (np.float32(500000.0) ** (-np.arange(half, dtype=np.float32) / np.float32(half))).astype(np.float32)
    rotc = np.zeros((128, 2), np.float32)
    for p in range(128):
        d = p % 64
        if d < 16:
            rotc[p, 0] = inv_freq[d % 8]
            rotc[p, 1] = -1.0 if d < 8 else 1.0
    sinkrow = np.zeros((1, 256), np.float32)
    sinkrow[0, 64:128] = 1.0
    sinkrow[0, 128:192] = 1.0
    return dict(ident=ident, perm=perm, maskb=np.ascontiguousarray(maskb), maskm=np.ascontiguousarray(maskm), pos=pos, rotc=rotc, sinkrow=sinkrow)


def _layout_params(p, n_real):
    f = lambda a: np.ascontiguousarray(np.asarray(a, dtype=np.float32))
    ap = attn_perm()
    w_in = f(p["w_in"]).copy()
    w_in[:, :, 1536:2560] = w_in[:, :, 1536:2560][:, :, ap]
    w_in[:, :, 3072:4096] = w_in[:, :, 3072:4096][:, :, ap]
    w_in_r = np.ascontiguousarray(w_in.reshape(2, NKC, 128, 20, 256).transpose(0, 3, 2, 1, 4)).reshape(2, 20, 128, NKC * 256)
    w_out = f(p["w_out"]).copy()
    w_out[:, 512:1536, :] = w_out[:, 512:1536, :][:, ap, :]
    w_out_r = np.ascontiguousarray(w_out.reshape(2, NKC, 128, 8, 256).transpose(0, 3, 2, 1, 4)).reshape(2, 8, 128, NKC * 256)
    pw_r = np.ascontiguousarray(f(p["conv_pw_w"]).reshape(2, 4, 128, 512).transpose(0, 2, 1, 3)).reshape(2, 128, 2048)
    bd = np.zeros((2, 128, 2, 4, 128), np.float32)
    for gi, nm in enumerate(("lru_wa", "lru_wx")):
        w = f(p[nm])
        for c in range(4):
            for hh in range(2):
                bd[:, hh * 64:(hh + 1) * 64, gi, c, hh * 64:(hh + 1) * 64] = w[:, 2 * c + hh]
    lru_bd = bd.reshape(2, 128, 1024)
    pc = np.zeros((128, 2, 4, NPCL), np.float32)

    def chan(a):
        return f(a).reshape(2, 4, 128).transpose(2, 0, 1)
    pc[:, :, :, 0:31] = f(p["conv_dw_w"]).reshape(2, 31, 4, 128).transpose(3, 0, 2, 1)
    pc[:, :, :, 31] = chan(p["conv_dw_b"])
    pc[:, :, :, 32] = chan(p["conv_ln_g"])
    pc[:, :, :, 33] = chan(p["conv_ln_b"])
    pc[:, :, :, 34] = chan(p["conv_pw_b"])
    pc[:, :, :, 35:39] = f(p["lru_conv_w"]).reshape(2, 4, 4, 128).transpose(3, 0, 2, 1)
    pc[:, :, :, 39] = chan(p["lru_conv_b"])
    pc[:, :, :, 40] = chan(p["lru_ba"])
    pc[:, :, :, 41] = chan(p["lru_bx"])
    pc[:, :, :, 42] = chan(p["lru_lambda"])
    pc = pc.reshape(128, 2 * 4 * NPCL)
    sinks = f(p["attn_sinks"])
    sk = np.zeros((2, 4, 4), np.float32)
    for l in range(2):
        for j in range(4):
            for g in range(4):
                sk[l, j, g] = sinks[l, head_of(4 * (j // 2) + g, j % 2)]
    sinks_rep = sk.reshape(1, 32)
    lnp = np.zeros((3, 128, 2, D), np.float32)
    lnp[0, :, 0, :] = f(p["ln_in_g"])[None]
    lnp[0, :, 1, :] = f(p["ln_in_b"])[None]
    for l in range(2):
        lnp[1 + l, :, 0, :] = f(p["ln_post_g"])[l][None]
        lnp[1 + l, :, 1, :] = f(p["ln_post_b"])[l][None]
    out = dict(meta=f(p["meta_tokens"]), lnp=lnp, w_in_r=w_in_r, w_out_r=w_out_r, pw_r=pw_r, lru_bd=lru_bd, pc=pc, sinks_rep=sinks_rep)
    out.update(_host_consts(NMETA + n_real * TT))
    return out


def run(inputs, n_real=8, dbg=False):
    x = np.asarray(inputs["x"], dtype=np.float32)
    nb = x.shape[0]
    shared = _layout_params(inputs, n_real)
    nc = build(n_real=n_real, dbg=dbg)
    in_maps = []
    for b in range(nb):
        m = dict(shared)
        m["x"] = np.ascontiguousarray(x[b, :n_real * TT])
        in_maps.append(m)
    res = run_bass_kernel_spmd(nc, in_maps, core_ids=list(range(nb)))
    return res


def kernel(**inputs):
    res = run(inputs, n_real=SEQ // TT)
    return np.stack([np.asarray(r["y"], dtype=np.float32) for r in res.results], axis=0)
```
